# Optimizing a Trainium2 kernel written in Bass

```python
import jax, jax.numpy as jnp
from jax import lax
import numpy as np

D_MODEL = 1024
BATCH = 16
SEQ = 256
DEPTH = 2
DEC_BATCH = 4
DEC_SEQ = 1024
PAST_LEN = 512

GRID_W = 64
MIX_WIDTH = 1024
MLA_HEADS = 4
NOPE_DIM = 128
ROPE_DIM = 64
V_DIM = 128
QK_DIM = NOPE_DIM + ROPE_DIM
MLA_WIDTH = MLA_HEADS * V_DIM
Q_LORA = 384
KV_LORA = 256
POOL_GROUPS = 4
POOL_GROUP_DIM = 64
POOL_WIDTH = POOL_GROUPS * POOL_GROUP_DIM
POOL_WINDOWS = (2, 4, 8, 16)
CONV_WIDTH = 256
CONV_K = 3
IN_SPLITS = (Q_LORA, KV_LORA, ROPE_DIM, MLA_WIDTH, POOL_WIDTH, POOL_WIDTH,
             CONV_WIDTH, CONV_WIDTH, CONV_WIDTH, CONV_WIDTH)
IN_WIDTH = sum(IN_SPLITS)
ROPE_BASE = 10000.0
AXIS_DIM = ROPE_DIM // 2
Q_BLOCK = 128
ATTN_SCALE = QK_DIM ** -0.5
EPS = 1e-6

kernel_name = "hybrid_mla_pool_conv_diffusion_step"


def rmsnorm(x, g):
    x32 = x.astype(jnp.float32)
    y = x32 * lax.rsqrt(jnp.mean(x32 * x32, axis=-1, keepdims=True) + EPS)
    return (y * g.astype(jnp.float32)).astype(x.dtype)


def axial_rope(L):
    rows = L // GRID_W
    row = jnp.repeat(jnp.arange(rows), GRID_W).astype(jnp.float32)
    col = jnp.tile(jnp.arange(GRID_W), rows).astype(jnp.float32)
    inv = 1.0 / (ROPE_BASE ** (jnp.arange(0, AXIS_DIM, 2, dtype=jnp.float32) / AXIS_DIM))
    ang = jnp.concatenate([row[:, None] * inv, col[:, None] * inv], axis=-1)
    return jnp.cos(ang), jnp.sin(ang)


def apply_rope(x, cos, sin):
    x32 = x.astype(jnp.float32)
    x1, x2 = x32[..., :AXIS_DIM], x32[..., AXIS_DIM:]
    return jnp.concatenate([x1 * cos - x2 * sin, x2 * cos + x1 * sin], axis=-1).astype(x.dtype)


def mla_attention(q_nope, q_rope, k_nope, k_rope, v):
    B, Lq, H, _ = q_nope.shape
    nb = Lq // Q_BLOCK

    def blocks(a):
        return jnp.moveaxis(a.reshape((B, nb, Q_BLOCK) + a.shape[2:]), 1, 0)

    def one(args):
        qn, qr = args
        s = (jnp.einsum('bqhd,bkhd->bhqk', qn, k_nope)
             + jnp.einsum('bqhr,bkr->bhqk', qr, k_rope)).astype(jnp.float32) * ATTN_SCALE
        p = jax.nn.softmax(s, axis=-1).astype(v.dtype)
        return jnp.einsum('bhqk,bkhd->bqhd', p, v)

    out = lax.map(one, (blocks(q_nope), blocks(q_rope)))
    return jnp.moveaxis(out, 0, 1).reshape(B, Lq, H * V_DIM)


def pool_mix(p, pool_w, pool_s):
    B, L, _ = p.shape
    cs = jnp.concatenate([jnp.zeros((B, 1, POOL_WIDTH), jnp.float32),
                          jnp.cumsum(p.astype(jnp.float32), axis=1)], axis=1)
    t = np.arange(L)
    outs = []
    for gi, w in enumerate(POOL_WINDOWS):
        lo = np.maximum(t - w // 2, 0)
        hi = np.minimum(t + (w - w // 2), L)
        cnt = (hi - lo).astype(np.float32)
        csg = cs[..., gi * POOL_GROUP_DIM:(gi + 1) * POOL_GROUP_DIM]
        outs.append((csg[:, hi] - csg[:, lo]) / cnt[None, :, None])
    pooled = (jnp.stack(outs, axis=2).astype(p.dtype)
              - p.reshape(B, L, POOL_GROUPS, POOL_GROUP_DIM))
    y = jnp.einsum('btgc,gcd->btgd', pooled, pool_w).reshape(B, L, POOL_WIDTH)
    return y * pool_s


def short_conv(z, conv_w):
    zp = jnp.pad(z, ((0, 0), (1, 1), (0, 0)))
    return zp[:, :-2] * conv_w[0] + zp[:, 1:-1] * conv_w[1] + zp[:, 2:] * conv_w[2]


def mixer_layer(x, mod, g_norm, w_in, g_q, w_uq, g_kv, w_ukv, pool_w, pool_s, conv_w, w_out,
                ctx=None, rope=None):
    B, L, _ = x.shape
    shift, scale, gate = jnp.split(mod, 3, axis=-1)
    h = rmsnorm(x, g_norm) * (1.0 + scale[:, None]) + shift[:, None]
    u = h @ w_in
    offs = [int(o) for o in np.cumsum(IN_SPLITS)[:-1]]
    cq, ckv_raw, kr, g_mla, px, g_pool, cb, cc, ch, g_conv = jnp.split(u, offs, axis=-1)

    q = (rmsnorm(cq, g_q) @ w_uq).reshape(B, L, MLA_HEADS, QK_DIM)
    q_nope, q_rope = q[..., :NOPE_DIM], q[..., NOPE_DIM:]
    ckv = rmsnorm(ckv_raw, g_kv)
    if rope is not None:
        cos, sin = rope
        q_rope = apply_rope(q_rope, cos[:, None], sin[:, None])
        kr = apply_rope(kr, cos, sin)
    if ctx is None:
        ckv_all, kr_all = ckv, kr
    else:
        ckv_all = jnp.concatenate([ctx[0], ckv], axis=1)
        kr_all = jnp.concatenate([ctx[1], kr], axis=1)
    kv = (ckv_all @ w_ukv).reshape(B, ckv_all.shape[1], MLA_HEADS, NOPE_DIM + V_DIM)
    k_nope, v = kv[..., :NOPE_DIM], kv[..., NOPE_DIM:]
    attn = mla_attention(q_nope, q_rope, k_nope, kr_all, v)

    pool = pool_mix(px, pool_w, pool_s)

    conv = cb * short_conv(cc * ch, conv_w)

    mixed = jnp.concatenate([jax.nn.silu(g_mla) * attn,
                             jax.nn.silu(g_pool) * pool,
                             jax.nn.silu(g_conv) * conv], axis=-1)
    y = x + gate[:, None] * (mixed @ w_out)
    return y, ckv, kr


def setup_inputs(seed: int = 0) -> dict:
    key = jax.random.key(seed)
    ks = jax.random.split(key, 24)
    f32 = jnp.float32
    nrm = lambda k, s, sc: jax.random.normal(k, s, f32) * sc
    gain = lambda k, s: 1.0 + 0.1 * jax.random.normal(k, s, f32)
    return {
        "x_prompt": nrm(ks[0], (BATCH, SEQ, D_MODEL), 1.0),
        "x_sample": nrm(ks[1], (DEC_BATCH, DEC_SEQ, D_MODEL), 1.0),
        "cache_ckv": nrm(ks[2], (DEC_BATCH, DEPTH, PAST_LEN, KV_LORA), 1.0),
        "cache_krope": nrm(ks[3], (DEC_BATCH, DEPTH, PAST_LEN, ROPE_DIM), 1.0),
        "c": nrm(ks[4], (DEC_BATCH, D_MODEL), 1.0),
        "c_ctx": nrm(ks[5], (D_MODEL,), 1.0),
        "w_mod": nrm(ks[6], (DEPTH, D_MODEL, 3 * D_MODEL), 0.5 * D_MODEL ** -0.5),
        "b_mod": nrm(ks[7], (DEPTH, 3 * D_MODEL), 0.02),
        "g_norm": gain(ks[8], (DEPTH, D_MODEL)),
        "w_in": nrm(ks[9], (DEPTH, D_MODEL, IN_WIDTH), D_MODEL ** -0.5),
        "g_q": gain(ks[10], (DEPTH, Q_LORA)),
        "w_uq": nrm(ks[11], (DEPTH, Q_LORA, MLA_HEADS * QK_DIM), Q_LORA ** -0.5),
        "g_kv": gain(ks[12], (DEPTH, KV_LORA)),
        "w_ukv": nrm(ks[13], (DEPTH, KV_LORA, MLA_HEADS * (NOPE_DIM + V_DIM)), KV_LORA ** -0.5),
        "pool_w": nrm(ks[14], (DEPTH, POOL_GROUPS, POOL_GROUP_DIM, POOL_GROUP_DIM), POOL_GROUP_DIM ** -0.5),
        "pool_s": gain(ks[15], (DEPTH, POOL_WIDTH)),
        "conv_w": nrm(ks[16], (DEPTH, CONV_K, CONV_WIDTH), CONV_K ** -0.5),
        "w_out": nrm(ks[17], (DEPTH, MIX_WIDTH, D_MODEL), MIX_WIDTH ** -0.5),
        "g_final": gain(ks[18], (D_MODEL,)),
    }


def reference(x_prompt, x_sample, cache_ckv, cache_krope, c, c_ctx, w_mod, b_mod, g_norm, w_in,
              g_q, w_uq, g_kv, w_ukv, pool_w, pool_s, conv_w, w_out, g_final):
    xp = x_prompt
    ckv_states, kr_states = [], []
    for l in range(DEPTH):
        mod_ctx = (jax.nn.silu(c_ctx)[None] @ w_mod[l] + b_mod[l])
        xp, ckv, kr = mixer_layer(xp, mod_ctx, g_norm[l], w_in[l], g_q[l], w_uq[l], g_kv[l], w_ukv[l],
                                  pool_w[l], pool_s[l], conv_w[l], w_out[l])
        ckv_states.append(ckv)
        kr_states.append(kr)
    y_prompt = rmsnorm(xp, g_final)
    state_ckv = jnp.stack(ckv_states, axis=1)
    state_krope = jnp.stack(kr_states, axis=1)

    rope = axial_rope(x_sample.shape[1])
    xs = x_sample
    for l in range(DEPTH):
        mod_lat = jax.nn.silu(c) @ w_mod[l] + b_mod[l]
        xs, _, _ = mixer_layer(xs, mod_lat, g_norm[l], w_in[l], g_q[l], w_uq[l], g_kv[l], w_ukv[l],
                               pool_w[l], pool_s[l], conv_w[l], w_out[l],
                               ctx=(cache_ckv[:, l], cache_krope[:, l]), rope=rope)
    y_sample = rmsnorm(xs, g_final)
    return (y_prompt, y_sample, state_ckv, state_krope)
```

```python
import contextlib
import numpy as np
import concourse.bass as bass
import concourse.mybir as mybir
from concourse.bass_utils import run_bass_kernel_spmd

F32 = mybir.dt.float32
BF = mybir.dt.bfloat16
AF = mybir.ActivationFunctionType
ALU = mybir.AluOpType

D = 1024
T = 1024
NKEY = 1536
DEPTH = 2
IN_W = 2752
ATTN_SCALE = 192 ** -0.5
EPS = 1e-6
BIG = 2000.0
NRING = 3
SAME_ENG_SYNC = True

ENGS = ("pe", "act", "dve", "pool", "sp")


class Op:
    __slots__ = ("eng", "fn", "deps", "signal", "dma_key", "total", "sigval", "idx")


class Sched:
    def __init__(self):
        self.ops = {e: [] for e in ENGS}
        self.lastw = {}
        self.readers = {}
        self.dma_count = {}
        self.out_keys = set()

    def add(self, eng, fn, reads=(), writes=(), dma_key=None, total=False, is_out=False):
        op = Op()
        op.eng, op.fn, op.signal, op.dma_key, op.total = eng, fn, False, dma_key, total
        op.sigval = None
        op.idx = len(self.ops[eng])
        deps = []
        ps_reads = [k for k in reads if k.startswith("ps")]
        if ps_reads:
            reads = [k for k in reads if not k.startswith("ps")]
            writes = list(writes) + ps_reads
        raw = set()
        for k in reads:
            w = self.lastw.get(k)
            if w is not None:
                deps.append(w)
                raw.add(id(w))
        for k in writes:
            w = self.lastw.get(k)
            if w is not None:
                deps.append(w)
            deps.extend(self.readers.get(k, ()))
        for k in reads:
            lst = self.readers.setdefault(k, [])
            if op.dma_key is None:
                lst[:] = [r for r in lst if not (r.dma_key is None and r.eng == eng)]
            lst.append(op)
        for k in writes:
            self.lastw[k] = op
            self.readers[k] = []
        final = []
        seen = set()
        for d in deps:
            if id(d) in seen or d is op:
                continue
            seen.add(id(d))
            if d.dma_key is None and d.eng == eng and op.dma_key is None:
                if eng == "pe" or not SAME_ENG_SYNC or id(d) not in raw:
                    continue
            if d.dma_key is not None and d.dma_key == dma_key:
                continue
            final.append((d, 16 * self.dma_count[d.dma_key] if d.dma_key is not None else None))
            d.signal = True
        op.deps = final
        if dma_key is not None:
            self.dma_count[dma_key] = self.dma_count.get(dma_key, 0) + 1
            op.sigval = 16 * self.dma_count[dma_key]
            if is_out:
                self.out_keys.add(dma_key)
        self.ops[eng].append(op)
        return op

    def emit(self, nc, stack):
        sems = {}
        for e in ENGS:
            sems[e] = stack.enter_context(nc.semaphore("s_" + e))
        for k in self.dma_count:
            sems["dma:" + k] = stack.enter_context(nc.semaphore("d_" + k))
        for e in ENGS:
            c = 0
            for op in self.ops[e]:
                if op.dma_key is None and op.signal:
                    c += 1
                    op.sigval = c
        block = stack.enter_context(nc.Block())
        sched = self

        def run(eng_name, engine):
            waited = {}
            for op in sched.ops[eng_name]:
                need = {}
                for d, dv in op.deps:
                    if d.dma_key is not None:
                        s = "dma:" + d.dma_key
                        v = 16 * sched.dma_count[d.dma_key] if d.total else dv
                    else:
                        s = d.eng
                        v = d.sigval
                    if need.get(s, 0) < v:
                        need[s] = v
                for s, v in need.items():
                    if waited.get(s, 0) < v:
                        engine.wait_ge(sems[s], v)
                        waited[s] = v
                ins = op.fn(engine)
                if op.dma_key is not None:
                    ins.then_inc(sems["dma:" + op.dma_key], 16)
                elif op.signal:
                    ins.then_inc(sems[eng_name], 1)
            if eng_name == "sp":
                for k in sorted(sched.out_keys):
                    engine.wait_ge(sems["dma:" + k], 16 * sched.dma_count[k])

        @block.tensor
        def _(e):
            run("pe", e)

        @block.scalar
        def _(e):
            run("act", e)

        @block.vector
        def _(e):
            run("dve", e)

        @block.gpsimd
        def _(e):
            run("pool", e)

        @block.sync
        def _(e):
            run("sp", e)


WIN_LOADS = [
    (0, 512, [("cq", 0, 0, 128), ("cq", 1, 128, 128), ("cq", 2, 256, 128), ("ckv", 0, 384, 128)]),
    (512, 448, [("ckv", 1, 512, 128), ("kr", 0, 640, 64), ("gm", 0, 704, 128), ("gm", 1, 832, 128)]),
    (960, 512, [("gm", 2, 960, 128), ("gm", 3, 1088, 128), ("px", 0, 1216, 256)]),
    (1472, 512, [("gp", 0, 1472, 128), ("gp", 1, 1600, 128), ("cb", 0, 1728, 128), ("cb", 1, 1856, 128)]),
    (1984, 512, [("cc", 0, 1984, 128), ("ch", 0, 2240, 128), ("cc", 1, 2112, 128), ("ch", 1, 2368, 128)]),
    (2496, 256, [("gc", 0, 2496, 128), ("gc", 1, 2624, 128)]),
]


def build_program():
    nc = bass.Bass("TRN2", target_bir_lowering=False)
    S = Sched()
    dr = {}

    def din(name, shape):
        dr[name] = nc.dram_tensor(name, list(shape), F32, kind="ExternalInput").ap()

    def dout(name, shape):
        dr[name] = nc.dram_tensor(name, list(shape), F32, kind="ExternalOutput").ap()

    din("xT", (D, T)); din("cvec", (128, 8)); din("ctx_ckvT", (DEPTH, 256, 512)); din("ctx_krT", (DEPTH, 64, 512))
    din("maskq", (4, T)); din("maskk", (4, NKEY)); din("cs", (64, 2, T)); din("poolA", (128, 32, 128))
    din("convfix", (128, 1))
    din("w_mod", (DEPTH, D, 3 * D)); din("b_modT", (128, DEPTH, 24)); din("g_normT", (128, DEPTH, 8))
    din("w_in", (DEPTH, D, IN_W)); din("g_qT", (128, DEPTH, 3)); din("w_uq", (DEPTH, 384, 768))
    din("g_kvT", (128, DEPTH, 2)); din("w_ukv", (DEPTH, 256, 1024)); din("pool_w", (DEPTH, 4, 64, 64))
    din("pool_sT", (128, DEPTH, 2)); din("conv_wT", (128, DEPTH, 2, 3)); din("w_out", (DEPTH, D, D))
    din("g_finalT", (128, 8))
    dout("yT", (D, T)); dout("ckvT_out", (DEPTH, 256, T)); dout("krT_out", (DEPTH, 64, T))

    stack = contextlib.ExitStack()
    with stack:
        def sb(name, shape, dt):
            return stack.enter_context(nc.sbuf_tensor(name, list(shape), dt))

        xT = sb("xT_s", (128, 8, T), F32)
        hT = sb("hT", (128, 8, T), BF)
        mixT = sb("mixT", (128, 8, T), BF)
        rs = [sb("rs%d" % i, (128, T), F32) for i in range(2)]
        ring = [sb("ring%d" % i, (128, 4096), BF) for i in range(NRING)]
        R3 = sb("R3", (128, 3072), F32)
        sqs = sb("sqs", (128, 5, T), BF)
        ckvT = sb("ckvT", (128, 2, T), F32)
        krT = sb("krT", (64, T), F32)
        pxtok = sb("pxtok", (128, 2048), BF)
        qrT = [sb("qrT%d" % i, (128, T), BF) for i in range(2)]
        ckvall = sb("ckvall", (128, 2, NKEY), BF)
        krall = [sb("krall%d" % i, (128, NKEY), BF) for i in range(DEPTH)]
        poolA = sb("poolA_s", (128, 32, 128), BF)
        pwpad = [sb("pwpad%d" % i, (64, 4, 128), BF) for i in range(DEPTH)]
        cs = sb("cs_s", (64, 2, T), F32)
        ropeA = sb("ropeA", (64, 512), F32)
        ropeS = sb("ropeS", (64, 512), F32)
        ones = sb("ones", (128, 128), BF)
        epsT = sb("epsT", (128, 1), F32)
        cvec = sb("cvec_s", (128, 8), F32)
        scb = sb("scb", (128, 8), BF)
        b_modT = sb("b_modT_s", (128, DEPTH, 24), F32)
        g_normT = sb("g_normT_s", (128, DEPTH, 8), F32)
        g_qT = sb("g_qT_s", (128, DEPTH, 3), F32)
        g_kvT = sb("g_kvT_s", (128, DEPTH, 2), F32)
        pool_sT = sb("pool_sT_s", (128, DEPTH, 2), F32)
        conv_wT = sb("conv_wT_s", (128, DEPTH, 2, 3), F32)
        g_finalT = sb("g_finalT_s", (128, 8), F32)
        convfix = sb("convfix_s", (128, 1), F32)
        nwb = sb("nwb", (128, DEPTH, 2, 2), F32)
        modT = sb("modT", (128, DEPTH, 24), F32)
        gs = sb("gs", (128, DEPTH, 8), F32)

        hflat = hT[:, :, :].rearrange("p a b -> p (a b)")
        vtok = hflat[:, 0:6144]
        pT = hflat[:, 6144:8192]
        cbT = sb("cbT", (128, 2048), F32)
        zT = sb("zT", (128, 2052), F32)
        knT = [sb("knT%d" % i, (128, NKEY), BF) for i in range(2)]
        qnT = [sb("qnT%d" % i, (128, T), BF) for i in range(2)]
        pooledT = sb("pooledT", (64, 4, 512), BF)
        ps2 = sb("ps2", (128, 3, 512), BF)

        def vkey(kt):
            return "hT%d_%d" % (kt // 2, kt % 2)

        def pkey(slot):
            return "hT%d_%d" % (6 + slot // 2, slot % 2)

        def r3(i):
            return R3[:, i * 1024:(i + 1) * 1024]

        psS = [stack.enter_context(nc.psum_tensor("psS%d" % i, [128, 1024], F32)) for i in range(2)]
        psB = stack.enter_context(nc.psum_tensor("psB", [128, 512], F32))
        psG = [psS[0][:, 0:512], psS[0][:, 512:1024], psS[1][:, 0:512], psS[1][:, 512:1024], psB[:, :]]
        BG_BANK = 4
        psO = [stack.enter_context(nc.psum_tensor("psO%d" % i, [128, 512], F32)) for i in range(2)]
        psD = stack.enter_context(nc.psum_tensor("psD", [128, 512], F32))

        st = {"g": 0, "ring": 0, "ring_order": [0, 1, 2]}

        def bank_general():
            i = st["g"] % 5
            st["g"] += 1
            return i

        def hsl(half):
            return slice(half * 512, (half + 1) * 512)

        small = [(cvec, "cvec"), (b_modT, "b_modT"), (g_normT, "g_normT"), (g_qT, "g_qT"), (g_kvT, "g_kvT"),
                 (pool_sT, "pool_sT"), (conv_wT, "conv_wT"), (g_finalT, "g_finalT"), (convfix, "convfix")]
        for t_, n_ in small[:3]:
            S.add("sp", lambda e, t_=t_, n_=n_: e.dma_start(out=t_[:], in_=dr[n_][:]),
                  writes=["c_" + n_], dma_key="const", total=True)
        for c in range(8):
            S.add("sp", lambda e, c=c: e.dma_start(out=xT[:, c, :], in_=dr["xT"][c * 128:(c + 1) * 128, :]),
                  writes=["xT%d_0" % c, "xT%d_1" % c], dma_key="x%d" % c)
        for t_, n_ in small[3:]:
            S.add("sp", lambda e, t_=t_, n_=n_: e.dma_start(out=t_[:], in_=dr[n_][:]),
                  writes=["c_" + n_], dma_key="const2", total=True)
        S.add("sp", lambda e: e.dma_start(out=cs[:], in_=dr["cs"][:]), writes=["c_cs"], dma_key="const2", total=True)
        S.add("dve", lambda e: e.memset(ones[:], 1.0), writes=["ones"])
        S.add("dve", lambda e: e.memset(epsT[:], EPS), writes=["eps"])
        for l in range(DEPTH):
            S.add("dve", lambda e, l=l: e.memset(pwpad[l][:], 0.0), writes=["pwpad%d_%d" % (l, g) for g in range(4)])
        S.add("dve", lambda e: e.memset(zT[:, :], 0.0), writes=["z0", "z1"])

        def ring_load(dmas, slot=None):
            if slot is None:
                s = st["ring_order"][st["ring"] % NRING]
                st["ring"] += 1
            else:
                s = slot
            key = "ring%d" % s
            for fn in dmas:
                S.add("pool", lambda e, fn=fn, s=s: fn(e, ring[s]), writes=[key], dma_key=key)
            return s, key

        def load_cols(src3, w):
            K = src3.shape[1]
            return [lambda e, slot, src3=src3, K=K, w=w: e.dma_start(
                out=slot[:, 0:K * w].rearrange("p (k n) -> p k n", k=K), in_=src3)]

        S.add("act", lambda e: e.activation(out=scb[:], in_=cvec[:], func=AF.Silu), reads=["c_cvec"], writes=["scb"])

        def mod_load(l, j, slot=None):
            src = dr["w_mod"][l].rearrange("(k p) n -> p k n", p=128)[:, :, j * 512:(j + 1) * 512]
            return ring_load(load_cols(src, 512), slot)

        def mod_part(l, j, bank=None, slot=None, loaded=None):
            s, key = loaded if loaded is not None else mod_load(l, j, slot)
            a = bank_general() if bank is None else bank
            pk = "psG%d" % a
            for m in range(4):
                for k in range(8):
                    S.add("pe", lambda e, s=s, m=m, k=k, a=a: e.matmul(
                        psG[a][:, m:m + 1], ring[s][:, k * 512 + m * 128:k * 512 + (m + 1) * 128],
                        scb[:, k:k + 1], start=(k == 0), stop=(k == 7)),
                        reads=[key, "scb"], writes=[pk])
            S.add("dve", lambda e, a=a: e.tensor_tensor(out=modT[:, l, j * 4:j * 4 + 4], in0=psG[a][:, 0:4],
                                                       in1=b_modT[:, l, j * 4:j * 4 + 4], op=ALU.add),
                  reads=[pk, "c_b_modT"], writes=["modT%d_%d" % (l, j)])
            if j == 3:
                S.add("dve", lambda e: e.scalar_tensor_tensor(out=gs[:, l, :], in0=modT[:, l, 8:16], scalar=1.0,
                                                              in1=g_normT[:, l, :], op0=ALU.add, op1=ALU.mult),
                      reads=["modT%d_2" % l, "modT%d_3" % l, "c_g_normT"], writes=["gs%d" % l])

        for l_ in range(DEPTH):
            for i in range(2):
                for j, kk in enumerate((0, 2)):
                    S.add("dve", lambda e, l_=l_, i=i, j=j, kk=kk: e.tensor_scalar(
                        out=nwb[:, l_, i, j:j + 1], in0=conv_wT[:, l_, i, kk:kk + 1], scalar1=convfix[:, 0:1],
                        scalar2=-1.0, op0=ALU.mult, op1=ALU.mult),
                        reads=["c_conv_wT", "c_convfix"], writes=["nwb"])

        def rstd_from(sq_aps_keys, half, nfeat, rs_t, rs_key):
            n = len(sq_aps_keys)
            a = bank_general()
            pk = "psG%d" % a
            for i, (ap, key) in enumerate(sq_aps_keys):
                S.add("pe", lambda e, ap=ap, i=i, n=n, a=a: e.matmul(psG[a][:, :], ones[:, :], ap, start=(i == 0),
                                                                    stop=(i == n - 1)),
                      reads=[key, "ones"], writes=[pk])
            S.add("act", lambda e, a=a: e.activation(out=rs_t[:, hsl(half)], in_=psG[a][:, :], func=AF.Ln,
                                                     scale=1.0 / nfeat, bias=epsT[:, 0:1]),
                  reads=[pk, "eps"], writes=[rs_key + str(half)])
            S.add("act", lambda e: e.activation(out=rs_t[:, hsl(half)], in_=rs_t[:, hsl(half)], func=AF.Exp,
                                                scale=-0.5),
                  reads=[rs_key + str(half)], writes=[rs_key + str(half)])

        def rope(ps_ap, ps_key, half, out_ap, out_key, copy_eng="dve"):
            if copy_eng == "dve":
                S.add("dve", lambda e: e.tensor_copy(out=ropeA[:, :], in_=ps_ap[0:64, :]),
                      reads=[ps_key], writes=["ropeA"])
            else:
                S.add("act", lambda e: e.activation(out=ropeA[:, :], in_=ps_ap[0:64, :], func=AF.Copy),
                      reads=[ps_key], writes=["ropeA"])
            S.add("dve", lambda e: e.tensor_copy(out=ropeS[0:32, :], in_=ropeA[32:64, :]),
                  reads=["ropeA"], writes=["ropeS"])
            S.add("dve", lambda e: e.tensor_copy(out=ropeS[32:64, :], in_=ropeA[0:32, :]),
                  reads=["ropeA"], writes=["ropeS"])
            S.add("dve", lambda e: e.tensor_tensor(out=ropeS[:, :], in0=ropeS[:, :], in1=cs[:, 1, hsl(half)],
                                                   op=ALU.mult),
                  reads=["ropeS", "c_cs"], writes=["ropeS"])
            S.add("dve", lambda e: e.tensor_tensor(out=ropeA[:, :], in0=ropeA[:, :], in1=cs[:, 0, hsl(half)],
                                                   op=ALU.mult),
                  reads=["ropeA", "c_cs"], writes=["ropeA"])
            S.add("dve", lambda e: e.tensor_tensor(out=out_ap, in0=ropeA[:, :], in1=ropeS[:, :], op=ALU.add),
                  reads=["ropeA", "ropeS"], writes=[out_key])

        def pool_consts():
            for hh in range(2):
                S.add("pool", lambda e, hh=hh: e.dma_start(out=qrT[hh][64:68, :], in_=dr["maskq"][:, :]),
                      writes=["qrm%d" % hh], dma_key="constp", total=True)
            for l_ in range(DEPTH):
                S.add("pool", lambda e, l_=l_: e.dma_start(out=krall[l_][64:68, :], in_=dr["maskk"][:, :]),
                      writes=["krm%d" % l_], dma_key="constp", total=True)
            S.add("pool", lambda e: e.dma_start(out=poolA[:], in_=dr["poolA"][:]), writes=["poolA"], dma_key="constp",
                  total=True)
            for ll in range(DEPTH):
                for g in range(4):
                    S.add("pool", lambda e, ll=ll, g=g: e.dma_start(
                        out=pwpad[ll][0:64, g, (g % 2) * 64:(g % 2) * 64 + 64], in_=dr["pool_w"][ll, g, :, :]),
                        writes=["pwpad%d_%d" % (ll, g)], dma_key="constp", total=True)


        def evac(l, kind, idx, half, ps, pk):
            h_ = hsl(half)
            if kind == "cq":
                S.add("dve", lambda e: e.tensor_copy(out=r3(idx)[:, h_], in_=ps[:, :]),
                      reads=[pk], writes=["R3_%d_%d" % (idx, half)])
                S.add("act", lambda e: e.activation(out=sqs[:, idx, h_], in_=r3(idx)[:, h_], func=AF.Square),
                      reads=["R3_%d_%d" % (idx, half)], writes=["sqs%d_%d" % (idx, half)])
            elif kind == "ckv":
                S.add("dve", lambda e: e.tensor_copy(out=ckvT[:, idx, h_], in_=ps[:, :]),
                      reads=[pk], writes=["ckvT%d_%d" % (idx, half)])
                S.add("act", lambda e: e.activation(out=sqs[:, 3 + idx, h_], in_=ckvT[:, idx, h_], func=AF.Square),
                      reads=["ckvT%d_%d" % (idx, half)], writes=["sqs%d_%d" % (3 + idx, half)])
            elif kind == "kr":
                rope(ps, pk, half, krT[:, h_], "krT%d" % half)
                S.add("dve", lambda e: e.tensor_copy(out=krall[l][0:64, 512 + half * 512: 1024 + half * 512],
                                                     in_=krT[:, h_]),
                      reads=["krT%d" % half], writes=["krall_cur%d_%d" % (l, half)])
            elif kind in ("gm", "gp", "gc"):
                ch_ = {"gm": 0, "gp": 4, "gc": 6}[kind] + idx
                S.add("act", lambda e: e.activation(out=mixT[:, ch_, h_], in_=ps[:, :], func=AF.Silu),
                      reads=[pk], writes=["mix%d_%d" % (ch_, half)])
            elif kind == "cb":
                S.add("dve", lambda e: e.tensor_copy(
                    out=cbT[:, idx * 1024 + half * 512: idx * 1024 + half * 512 + 512], in_=ps[:, :]),
                    reads=[pk], writes=["cb%d" % idx])
            elif kind == "cc":
                S.add("act", lambda e: e.activation(out=r3(1)[:, h_], in_=ps[:, :], func=AF.Copy),
                      reads=[pk], writes=["R3_1_%d" % half])
            elif kind == "ch":
                S.add("dve", lambda e: e.tensor_tensor(
                    out=zT[:, idx * 1026 + 1 + half * 512: idx * 1026 + 1 + half * 512 + 512], in0=ps[:, :],
                    in1=r3(1)[:, h_], op=ALU.mult),
                    reads=[pk, "R3_1_%d" % half], writes=["z%d" % idx])

        def xnorm_stats(l):
            for half in range(2):
                xnorm_stats_half(l, half)

        def xnorm_stats_half(l, half):
            for _ in range(1):
                for c in range(8):
                    S.add("act", lambda e, c=c, half=half: e.activation(out=mixT[:, c, hsl(half)],
                                                                        in_=xT[:, c, hsl(half)], func=AF.Square),
                          reads=["xT%d_%d" % (c, half)], writes=["mix%d_%d" % (c, half)])
                rstd_from([(mixT[:, c, hsl(half)], "mix%d_%d" % (c, half)) for c in range(8)], half, D, rs[0], "rsx")
                if l == 0:
                    for c in range(8):
                        S.add("dve", lambda e, c=c, half=half: e.tensor_tensor(
                            out=hT[:, c, hsl(half)], in0=xT[:, c, hsl(half)], in1=rs[0][:, hsl(half)], op=ALU.mult),
                            reads=["xT%d_%d" % (c, half), "rsx%d" % half], writes=["hT%d_%d" % (c, half)])

        def xnorm_apply(l, halves=(0, 1)):
            for half in halves:
                for c in range(8):
                    xnorm_apply_one(l, half, c)

        def xnorm_apply_one(l, half, c):
            for _ in range(1):
                for __ in range(1):
                    hk = "hT%d_%d" % (c, half)
                    mk = ["gs%d" % l, "modT%d_%d" % (l, c // 4)]
                    if l == 0:
                        if c % 4 == 0:
                            S.add("act", lambda e, c=c, half=half: e.activation(
                                out=hT[:, c, hsl(half)], in_=hT[:, c, hsl(half)], func=AF.Identity,
                                scale=gs[:, l, c:c + 1], bias=modT[:, l, c:c + 1]),
                                reads=[hk] + mk, writes=[hk])
                        else:
                            S.add("dve", lambda e, c=c, half=half: e.tensor_scalar(
                                out=hT[:, c, hsl(half)], in0=hT[:, c, hsl(half)], scalar1=gs[:, l, c:c + 1],
                                scalar2=modT[:, l, c:c + 1], op0=ALU.mult, op1=ALU.add),
                                reads=[hk] + mk, writes=[hk])
                    else:
                        par = c % 2
                        tb = r3(par)[:, hsl(half)]
                        tk = "R3_%d_%d" % (par, half)
                        S.add("dve", lambda e, c=c, half=half, tb=tb: e.tensor_tensor(
                            out=tb, in0=xT[:, c, hsl(half)], in1=rs[0][:, hsl(half)], op=ALU.mult),
                            reads=["xT%d_%d" % (c, half), "rsx%d" % half], writes=[tk])
                        S.add("act", lambda e, c=c, half=half, tb=tb: e.activation(
                            out=hT[:, c, hsl(half)], in_=tb, func=AF.Identity, scale=gs[:, l, c:c + 1],
                            bias=modT[:, l, c:c + 1]),
                            reads=[tk] + mk, writes=[hk])


        def final_stats_half(half):
            for c in range(8):
                S.add("act", lambda e, c=c: e.activation(out=mixT[:, c, hsl(half)], in_=xT[:, c, hsl(half)],
                                                         func=AF.Square),
                      reads=["xT%d_%d" % (c, half)], writes=["mix%d_%d" % (c, half)])
            rstd_from([(mixT[:, c, hsl(half)], "mix%d_%d" % (c, half)) for c in range(8)], half, D, rs[0], "rsx")

        def final_apply_one(half, c):
            S.add("dve", lambda e: e.scalar_tensor_tensor(
                out=xT[:, c, hsl(half)], in0=xT[:, c, hsl(half)], scalar=g_finalT[:, c:c + 1],
                in1=rs[0][:, hsl(half)], op0=ALU.mult, op1=ALU.mult),
                reads=["xT%d_%d" % (c, half), "rsx%d" % half, "c_g_finalT"], writes=["xT%d_%d" % (c, half)])
            S.add("sp", lambda e: e.dma_start(out=dr["yT"][c * 128:(c + 1) * 128, hsl(half)], in_=xT[:, c, hsl(half)]),
                  reads=["xT%d_%d" % (c, half)], dma_key="yout", total=True, is_out=True)

        preloaded = {}

        def layer(l):
            def ctx_loads():
                for k in range(2):
                    S.add("pool", lambda e, k=k: e.dma_start(out=ckvall[:, k, 0:512],
                                                              in_=dr["ctx_ckvT"][l, k * 128:(k + 1) * 128, :]),
                          writes=["ckvall_ctx"], dma_key="ctxc%d" % l)
                S.add("pool", lambda e: e.dma_start(out=krall[l][0:64, 0:512], in_=dr["ctx_krT"][l, :, :]),
                      writes=["krall_ctx%d" % l], dma_key="ctxk%d" % l)

            if l > 0:
                ctx_loads()

            if l > 0:
                xnorm_stats_half(l, 1)
                xnorm_apply(l, halves=(1,))
            else:
                xnorm_apply(l)
            st["g"] += 4

            def norms():
                for half in range(2):
                    rstd_from([(sqs[:, i, hsl(half)], "sqs%d_%d" % (i, half)) for i in range(3)], half, 384,
                              rs[1], "rsq")
                for i in range(3):
                    S.add("dve", lambda e, i=i: e.scalar_tensor_tensor(
                        out=sqs[:, i, :], in0=r3(i), scalar=g_qT[:, l, i:i + 1], in1=rs[1][:, :],
                        op0=ALU.mult, op1=ALU.mult),
                        reads=["R3_%d_0" % i, "R3_%d_1" % i, "rsq0", "rsq1", "c_g_qT"],
                        writes=["sqs%d_0" % i, "sqs%d_1" % i])
                for half in range(2):
                    rstd_from([(sqs[:, 3 + i, hsl(half)], "sqs%d_%d" % (3 + i, half)) for i in range(2)], half,
                              256, rs[0], "rsx")
                for i in range(2):
                    S.add("dve", lambda e, i=i: e.scalar_tensor_tensor(
                        out=ckvT[:, i, :], in0=ckvT[:, i, :], scalar=g_kvT[:, l, i:i + 1], in1=rs[0][:, :],
                        op0=ALU.mult, op1=ALU.mult),
                        reads=["ckvT%d_0" % i, "ckvT%d_1" % i, "rsx0", "rsx1", "c_g_kvT"],
                        writes=["ckvT%d_0" % i, "ckvT%d_1" % i])
                    S.add("dve", lambda e, i=i: e.tensor_copy(out=ckvall[:, i, 512:NKEY], in_=ckvT[:, i, :]),
                          reads=["ckvT%d_0" % i, "ckvT%d_1" % i], writes=["ckvall_cur%d" % i])
                    S.add("sp", lambda e, i=i: e.dma_start(out=dr["ckvT_out"][l, i * 128:(i + 1) * 128, :],
                                                           in_=ckvT[:, i, :]),
                          reads=["ckvT%d_0" % i, "ckvT%d_1" % i], dma_key="ckvo%d" % l, is_out=True)
                S.add("sp", lambda e: e.dma_start(out=dr["krT_out"][l, :, :], in_=krT[:, :]),
                      reads=["krT0", "krT1"], dma_key="kro%d" % l, is_out=True)

            conv_ops = []
            _S_add = S.add

            def _defer(*a, **k):
                conv_ops.append(lambda: _S_add(*a, **k))

            def conv_chunk(i):
                acc = r3(0) if i == 0 else r3(2)
                akey = ["R3_%d_0" % (0 if i == 0 else 2), "R3_%d_1" % (0 if i == 0 else 2)]
                zb = i * 1026
                _defer("dve", lambda e, i=i, zb=zb: e.tensor_scalar(
                    out=acc, in0=zT[:, zb + 1:zb + 1025], scalar1=conv_wT[:, l, i, 1:2], scalar2=None,
                    op0=ALU.mult),
                    reads=["z%d" % i, "c_conv_wT"], writes=akey)
                _defer("dve", lambda e, i=i, zb=zb: e.scalar_tensor_tensor(
                    out=acc, in0=zT[:, zb:zb + 1024], scalar=conv_wT[:, l, i, 0:1], in1=acc,
                    op0=ALU.mult, op1=ALU.add),
                    reads=["z%d" % i] + akey, writes=akey)
                _defer("dve", lambda e, i=i, zb=zb: e.scalar_tensor_tensor(
                    out=acc, in0=zT[:, zb + 2:zb + 1026], scalar=conv_wT[:, l, i, 2:3], in1=acc,
                    op0=ALU.mult, op1=ALU.add),
                    reads=["z%d" % i] + akey, writes=akey)
                _defer("dve", lambda e, i=i, zb=zb: e.scalar_tensor_tensor(
                    out=acc[:, 256:1024:256], in0=zT[:, zb + 256:zb + 1024:256], scalar=nwb[:, l, i, 0:1],
                    in1=acc[:, 256:1024:256], op0=ALU.mult, op1=ALU.add),
                    reads=["z%d" % i, "nwb"] + akey, writes=akey)
                _defer("dve", lambda e, i=i, zb=zb: e.scalar_tensor_tensor(
                    out=acc[:, 255:1023:256], in0=zT[:, zb + 257:zb + 1025:256], scalar=nwb[:, l, i, 1:2],
                    in1=acc[:, 255:1023:256], op0=ALU.mult, op1=ALU.add),
                    reads=["z%d" % i, "nwb"] + akey, writes=akey)
                _defer("dve", lambda e, i=i: e.tensor_tensor(out=acc, in0=acc,
                                                            in1=cbT[:, i * 1024:(i + 1) * 1024], op=ALU.mult),
                      reads=akey + ["cb%d" % i], writes=akey)
                _defer("dve", lambda e, i=i: e.tensor_tensor(out=mixT[:, 6 + i, :], in0=acc, in1=mixT[:, 6 + i, :],
                                                            op=ALU.mult),
                      reads=akey + ["mix%d_0" % (6 + i), "mix%d_1" % (6 + i)],
                      writes=["mix%d_0" % (6 + i), "mix%d_1" % (6 + i)])

            for i_ in range(2):
                conv_chunk(i_)
            conv_gate = [conv_ops.pop(6), conv_ops.pop(12)]
            conv_avail = {"n": 0}

            for li, (c0, w, chunks) in enumerate(WIN_LOADS):
                src = dr["w_in"][l].rearrange("(k p) n -> p k n", p=128)[:, :, c0:c0 + w]
                if (l, li) in preloaded:
                    s, key = preloaded[(l, li)]
                    st["ring"] += 1
                else:
                    s, key = ring_load(load_cols(src, w))
                if li == 0:
                    items = [(ch_, hf_) for hf_ in range(2) for ch_ in chunks]
                else:
                    items = [(ch_, hf_) for ch_ in chunks for hf_ in ((None,) if ch_[0] == "px" else (0, 1))]
                for ((kind, idx, col, cw), half_sel) in items:
                    off = col - c0
                    if kind == "px":
                        for jp in range(4):
                            a = bank_general()
                            pk = "psG%d" % a
                            for jj in range(2):
                                j = jp * 2 + jj
                                for k in range(8):
                                    S.add("pe", lambda e, a=a, jj=jj, j=j, k=k, s=s, off=off, w=w: e.matmul(
                                        psG[a][:, jj * 256:(jj + 1) * 256], hT[:, k, j * 128:(j + 1) * 128],
                                        ring[s][:, k * w + off:k * w + off + 256], start=(k == 0), stop=(k == 7)),
                                        reads=[key, "hT%d_%d" % (k, j // 4)], writes=[pk])
                            S.add("dve", lambda e, a=a, jp=jp: e.tensor_copy(
                                out=pxtok[:, jp * 512:(jp + 1) * 512], in_=psG[a][:, :]),
                                reads=[pk], writes=["pxtok%d" % jp])
                        continue
                    for half in (half_sel,):
                        a = bank_general()
                        pk = "psG%d" % a
                        for k in range(8):
                            S.add("pe", lambda e, a=a, k=k, s=s, off=off, w=w, cw=cw, half=half: e.matmul(
                                psG[a][0:cw, :], ring[s][:, k * w + off:k * w + off + cw], hT[:, k, hsl(half)],
                                start=(k == 0), stop=(k == 7)),
                                reads=[key, "hT%d_%d" % (k, half)], writes=[pk])
                        evac(l, kind, idx, half, psG[a], pk)
                        if kind == "ch" and half == 1:
                            conv_avail["n"] += 6
                        for _ in range(2):
                            if conv_avail["n"] > 0 and conv_ops and (kind in ("ch", "gc")):
                                conv_ops.pop(0)()
                                conv_avail["n"] -= 1
                if li == 1 and l == 0:
                    pool_consts()
                    ctx_loads()
                if li == 3:
                    norms()

            while conv_ops:
                conv_ops.pop(0)()
            for f in conv_gate:
                f()

            srcq = dr["w_uq"][l].rearrange("(k p) n -> p k n", p=128)
            sq_, keyq = ring_load(load_cols(srcq, 768))
            srck = dr["w_ukv"][l].rearrange("(k p) (h t d) -> p k t h d", p=128, h=4, t=2)
            s_kv = st["ring_order"][st["ring"] % NRING]
            keykv = "ring%d" % s_kv
            st["ring"] += 1
            for t_ in range(2):
                for k_ in range(2):
                    S.add("pool", lambda e, t_=t_, k_=k_: e.dma_start(
                        out=ring[s_kv][:, 0:2048].rearrange("p (k t h d) -> p k t h d", k=2, t=2, h=4)[:, k_, t_],
                        in_=srck[:, k_, t_]),
                        writes=[keykv], dma_key=keykv)

            def wk_ap(k, h):
                o = k * 1024 + h * 128
                return ring[s_kv][:, o:o + 128]

            def wv_ap(k):
                o = k * 1024 + 512
                return ring[s_kv][:, o:o + 512]

            def v_tile(kt):
                def f(bank=None):
                    a = bank_general() if bank is None else bank
                    pk = "psG%d" % a
                    for k in range(2):
                        S.add("pe", lambda e, k=k: e.matmul(
                            psG[a][:, :], ckvall[:, k, kt * 128:(kt + 1) * 128], wv_ap(k), start=(k == 0),
                            stop=(k == 1)),
                            reads=[keykv, "ckvall_ctx" if kt < 4 else "ckvall_cur%d" % k], writes=[pk])
                    if kt % 2 == 0:
                        S.add("dve", lambda e: e.tensor_copy(out=vtok[:, kt * 512:(kt + 1) * 512], in_=psG[a][:, :]),
                              reads=[pk], writes=[vkey(kt)])
                    else:
                        S.add("act", lambda e: e.activation(out=vtok[:, kt * 512:(kt + 1) * 512], in_=psG[a][:, :],
                                                            func=AF.Copy),
                              reads=[pk], writes=[vkey(kt)])
                return f

            def diag_blk(j, g):
                return (0 if j == 0 else 4 if j == 7 else 8 if j % 2 == 0 else 12) + g

            def pool_tile(jh, g):
                def f(bank=None):
                    a = bank_general() if bank is None else bank
                    pk = "psG%d" % a
                    for jj in range(4):
                        j = jh * 4 + jj
                        terms = [(j, diag_blk(j, g))]
                        if j > 0:
                            terms.append((j - 1, (24 if (j - 1) in (1, 3, 5) else 16) + g))
                        if j < 7:
                            terms.append((j + 1, (28 if j in (1, 3, 5) else 20) + g))
                        for ti, (i, blk) in enumerate(terms):
                            S.add("pe", lambda e, jj=jj, i=i, blk=blk, ti=ti, n=len(terms): e.matmul(
                                psG[a][0:64, jj * 128:(jj + 1) * 128], pxtok[:, i * 256 + g * 64:i * 256 + g * 64 + 64],
                                poolA[:, blk, :], start=(ti == 0), stop=(ti == n - 1)),
                                reads=["pxtok%d" % (i // 2), "poolA"], writes=[pk])
                    S.add("act", lambda e: e.activation(out=pooledT[:, g, :], in_=psG[a][0:64, :], func=AF.Copy),
                          reads=[pk], writes=["pooled%d" % g])
                return f

            def pooly_tile(half, c):
                def f(bank=None):
                    a = bank_general() if bank is None else bank
                    pk = "psG%d" % a
                    for gg in range(2):
                        g = 2 * c + gg
                        S.add("pe", lambda e, g=g, gg=gg: e.matmul(
                            psG[a][:, :], pwpad[l][0:64, g, :], pooledT[:, g, :], start=(gg == 0), stop=(gg == 1)),
                            reads=["pwpad%d_%d" % (l, g), "pooled%d" % g], writes=[pk])
                    S.add("dve", lambda e: e.scalar_tensor_tensor(
                        out=mixT[:, 4 + c, hsl(half)], in0=psG[a][:, :], scalar=pool_sT[:, l, c:c + 1],
                        in1=mixT[:, 4 + c, hsl(half)], op0=ALU.mult, op1=ALU.mult),
                        reads=[pk, "c_pool_sT", "mix%d_%d" % (4 + c, half)], writes=["mix%d_%d" % (4 + c, half)])
                return f

            def prep_tiles(h, fg=False):
                hh = h % 2
                tiles = []

                def qn(half):
                    def f(bank=None):
                        a = bank_general() if bank is None else bank
                        pk = "psG%d" % a
                        for k in range(3):
                            S.add("pe", lambda e, k=k: e.matmul(
                                psG[a][:, :], ring[sq_][:, k * 768 + h * 192:k * 768 + h * 192 + 128],
                                sqs[:, k, hsl(half)], start=(k == 0), stop=(k == 2)),
                                reads=[keyq, "sqs%d_%d" % (k, half)], writes=[pk])
                        if fg:
                            S.add("act", lambda e: e.activation(out=qnT[hh][:, hsl(half)], in_=psG[a][:, :],
                                                                func=AF.Copy),
                                  reads=[pk], writes=["qnT%d_%d" % (hh, half)])
                        else:
                            S.add("dve", lambda e: e.tensor_copy(out=qnT[hh][:, hsl(half)], in_=psG[a][:, :]),
                                  reads=[pk], writes=["qnT%d_%d" % (hh, half)])
                    return f

                def qr(half):
                    def f(bank=None):
                        a = bank_general() if bank is None else bank
                        pk = "psG%d" % a
                        for k in range(3):
                            S.add("pe", lambda e, k=k: e.matmul(
                                psG[a][0:64, :], ring[sq_][:, k * 768 + h * 192 + 128:k * 768 + h * 192 + 192],
                                sqs[:, k, hsl(half)], start=(k == 0), stop=(k == 2)),
                                reads=[keyq, "sqs%d_%d" % (k, half)], writes=[pk])
                        rope(psG[a], pk, half, qrT[hh][0:64, hsl(half)], "qrT%d_%d" % (hh, half),
                             "act" if fg else "dve")
                    return f

                def kn(kb):
                    def f(bank=None):
                        a = bank_general() if bank is None else bank
                        pk = "psG%d" % a
                        for k in range(2):
                            S.add("pe", lambda e, k=k: e.matmul(
                                psG[a][:, :], wk_ap(k, h), ckvall[:, k, kb * 512:(kb + 1) * 512], start=(k == 0),
                                stop=(k == 1)),
                                reads=[keykv, "ckvall_ctx" if kb == 0 else "ckvall_cur%d" % k], writes=[pk])
                        if fg:
                            S.add("act", lambda e: e.activation(out=knT[hh][:, kb * 512:(kb + 1) * 512],
                                                                in_=psG[a][:, :], func=AF.Copy),
                                  reads=[pk], writes=["knT%d_%d" % (hh, kb)])
                        else:
                            S.add("dve", lambda e: e.tensor_copy(out=knT[hh][:, kb * 512:(kb + 1) * 512],
                                                                 in_=psG[a][:, :]),
                                  reads=[pk], writes=["knT%d_%d" % (hh, kb)])
                    return f

                return [qn(0), kn(0), kn(1), qr(0), qn(1), kn(2), qr(1)]

            free_slot = [i for i in range(NRING) if i not in (sq_, s_kv)][0]

            mod_jobs = []
            if l == 0:
                mod_jobs += [(0, 4), (0, 5)]
            if l + 1 < DEPTH:
                mod_jobs += [(l + 1, j) for j in range(6)]
            mod_state = {"next": 0, "loaded": None}

            def mod_prefetch():
                i = mod_state["next"]
                if i < len(mod_jobs):
                    mod_state["loaded"] = mod_load(mod_jobs[i][0], mod_jobs[i][1], free_slot)

            def mod_step(bank):
                i = mod_state["next"]
                if i < len(mod_jobs):
                    mod_part(mod_jobs[i][0], mod_jobs[i][1], bank, free_slot, mod_state["loaded"])
                    mod_state["next"] = i + 1
                    mod_prefetch()

            pending = {"fin": None, "fin_a": None}

            def unit(h, qh, bg):
                hh = h % 2
                u = h * 2 + qh
                ob = u % 2

                def Spair(jp):
                    for t in range(2):
                        kt = 2 * jp + t
                        a = 2 * (jp % 2) + t
                        pk = "psG%d" % a
                        S.add("pe", lambda e, kt=kt, a=a: e.matmul(
                            psG[a][:, :], knT[hh][:, kt * 128:(kt + 1) * 128], qnT[hh][:, hsl(qh)], start=True,
                            stop=False),
                            reads=["knT%d_%d" % (hh, kt // 4), "qnT%d_%d" % (hh, qh)], writes=[pk])
                        S.add("pe", lambda e, kt=kt, a=a: e.matmul(
                            psG[a][:, :], krall[l][0:68, kt * 128:(kt + 1) * 128], qrT[hh][0:68, hsl(qh)],
                            start=False, stop=True),
                            reads=["krall_ctx%d" % l if kt < 4 else "krall_cur%d_%d" % (l, (kt - 4) // 4),
                                   "krm%d" % l, "qrT%d_%d" % (hh, qh), "qrm%d" % hh], writes=[pk])

                def Epair(jp):
                    p = jp % 2
                    S.add("act", lambda e: e.activation(
                        out=pT[:, p * 1024:(p + 1) * 1024], in_=psS[p][:, :], func=AF.Exp, scale=ATTN_SCALE),
                        reads=["psG%d" % (2 * p), "psG%d" % (2 * p + 1)], writes=[pkey(2 * p), pkey(2 * p + 1)])

                def Vpair(jp):
                    for t in range(2):
                        kt = 2 * jp + t
                        slot = 2 * (jp % 2) + t
                        S.add("pe", lambda e, kt=kt, slot=slot: e.matmul(
                            psO[ob][:, :], vtok[:, kt * 512 + h * 128:kt * 512 + (h + 1) * 128],
                            pT[:, slot * 512:(slot + 1) * 512], start=(kt == 0), stop=(kt == 11)),
                            reads=[vkey(kt), pkey(slot)], writes=["psO%d" % ob])

                def PSop(jp):
                    s0 = 2 * (jp % 2)
                    S.add("pool", lambda e: e.tensor_tensor(
                        out=ps2[:, jp % 3, :], in0=pT[:, s0 * 512:(s0 + 1) * 512],
                        in1=pT[:, (s0 + 1) * 512:(s0 + 2) * 512], op=ALU.add),
                        reads=[pkey(s0), pkey(s0 + 1)], writes=["ps2_%d" % (jp % 3)])

                def Dop(jp):
                    S.add("pe", lambda e: e.matmul(
                        psD[:, :], ones[:, :], ps2[:, jp % 3, :], start=(jp == 0), stop=(jp == 5)),
                        reads=["ones", "ps2_%d" % (jp % 3)], writes=["psD"])

                rd = r3(2)[:, ob * 512:(ob + 1) * 512]
                at = r3(0)[:, ob * 512:(ob + 1) * 512]
                rk = "R3_2_%d" % ob

                def finalize_a():
                    Dop(5)
                    S.add("act", lambda e: e.activation(out=rd, in_=psD[:, :], func=AF.Ln),
                          reads=["psD"], writes=[rk])

                def finalize():
                    S.add("act", lambda e: e.activation(out=rd, in_=rd, func=AF.Exp, scale=-1.0),
                          reads=[rk], writes=[rk])
                    S.add("dve", lambda e: e.tensor_tensor(out=at, in0=psO[ob][:, :], in1=rd, op=ALU.mult),
                          reads=["psO%d" % ob, rk], writes=["R3_0_%d" % ob])
                    S.add("dve", lambda e: e.tensor_tensor(out=mixT[:, h, hsl(qh)], in0=at, in1=mixT[:, h, hsl(qh)],
                                                           op=ALU.mult),
                          reads=["R3_0_%d" % ob, "mix%d_%d" % (h, qh)], writes=["mix%d_%d" % (h, qh)])

                Spair(0)
                Spair(1)
                for jp in range(6):
                    Epair(jp)
                    PSop(jp)
                    if jp == 0 and pending["fin_a"] is not None:
                        pending["fin_a"]()
                        pending["fin_a"] = None
                    if jp == 1 and pending["fin"] is not None:
                        pending["fin"]()
                        pending["fin"] = None
                    if jp + 2 < 6:
                        Spair(jp + 2)
                    Vpair(jp)
                    if jp >= 1:
                        Dop(jp - 1)
                    if jp < 4 and bg:
                        bg.pop(0)(BG_BANK)
                    if jp == 4:
                        mod_step(BG_BANK)
                pending["fin_a"] = finalize_a
                pending["fin"] = finalize

            fg = [v_tile(kt) for kt in range(12)]
            p0 = prep_tiles(0, fg=True)
            pl = []
            for jh in range(2):
                pl += [pool_tile(jh, g) for g in range(4)] + [pooly_tile(jh, c) for c in range(2)]
            pq = [p0[i] for i in (0, 3, 4, 6)]
            pk_ = [p0[i] for i in (1, 2, 5)]
            order = []
            while pl or pq:
                if pl:
                    order.append(pl.pop(0))
                if pq:
                    order.append(pq.pop(0))
                if pl:
                    order.append(pl.pop(0))
            while fg or pk_:
                for _ in range(4):
                    if fg:
                        order.append(fg.pop(0))
                if pk_:
                    order.append(pk_.pop(0))
            for f in order:
                f()

            mod_prefetch()
            bg = prep_tiles(1)
            unit(0, 0, bg)
            unit(0, 1, bg)
            bg += prep_tiles(2)
            unit(1, 0, bg)
            unit(1, 1, bg)
            bg += prep_tiles(3)
            unit(2, 0, bg)
            unit(2, 1, bg)
            assert not bg
            wo = []
            for ob_ in range(2):
                src = dr["w_out"][l].rearrange("(k p) n -> p k n", p=128)[:, :, ob_ * 512:(ob_ + 1) * 512]
                wo.append(ring_load(load_cols(src, 512), (sq_, s_kv)[ob_]))
            unit(3, 0, bg)
            unit(3, 1, bg)
            assert mod_state["next"] == len(mod_jobs)
            if l + 1 < DEPTH:
                c0, w, _ = WIN_LOADS[0]
                src = dr["w_in"][l + 1].rearrange("(k p) n -> p k n", p=128)[:, :, c0:c0 + w]
                preloaded[(l + 1, 0)] = ring_load(load_cols(src, w), free_slot)
            pending["fin_a"]()
            pending["fin"]()
            pending["fin_a"] = None
            pending["fin"] = None

            tix = 0
            for half in range(2):
                for ob_ in range(2):
                    s, key = wo[ob_]
                    for m in range(4):
                        oc = ob_ * 4 + m
                        if half == 1:
                            if tix == 0:
                                if l + 1 < DEPTH:
                                    xnorm_stats_half(l + 1, 0)
                                else:
                                    final_stats_half(0)
                            else:
                                for c_ in ((0, 1) if tix == 1 else (tix,)):
                                    if l + 1 < DEPTH:
                                        xnorm_apply_one(l + 1, 0, c_)
                                    else:
                                        final_apply_one(0, c_)
                            tix += 1
                        a = bank_general()
                        pk = "psG%d" % a
                        for k in range(8):
                            S.add("pe", lambda e, a=a, k=k, s=s, m=m, half=half: e.matmul(
                                psG[a][:, :], ring[s][:, k * 512 + m * 128:k * 512 + (m + 1) * 128],
                                mixT[:, k, hsl(half)], start=(k == 0), stop=(k == 7)),
                                reads=[key, "mix%d_%d" % (k, half)], writes=[pk])
                        S.add("dve", lambda e, a=a, oc=oc, half=half: e.scalar_tensor_tensor(
                            out=xT[:, oc, hsl(half)], in0=psG[a][:, :], scalar=modT[:, l, 16 + oc:17 + oc],
                            in1=xT[:, oc, hsl(half)], op0=ALU.mult, op1=ALU.add),
                            reads=[pk, "modT%d_%d" % (l, 4 + oc // 4), "xT%d_%d" % (oc, half)],
                            writes=["xT%d_%d" % (oc, half)])
            st["ring_order"] = [free_slot, sq_, s_kv]
            st["ring"] = 0

        xnorm_stats(0)
        for j in range(4):
            mod_part(0, j)
        for l in range(DEPTH):
            layer(l)

        final_stats_half(1)
        for c in range(8):
            final_apply_one(1, c)

        S.emit(nc, stack)
    return nc


POOL_WINDOWS = (2, 4, 8, 16)


def _pool_blocks(seq_len):
    L = seq_len
    blocks = np.zeros((32, 128, 128), np.float32)
    for g, w in enumerate(POOL_WINDOWS):
        A = np.zeros((1024, 1024), np.float32)
        for tt in range(1024):
            s0 = (tt // L) * L
            lo = max(tt - w // 2, s0)
            hi = min(tt + (w - w // 2), s0 + L)
            A[lo:hi, tt] = 1.0 / float(hi - lo)
            A[tt, tt] -= 1.0
        blocks[g] = A[0:128, 0:128]
        blocks[4 + g] = A[896:1024, 896:1024]
        blocks[8 + g] = A[256:384, 256:384]
        blocks[12 + g] = A[128:256, 128:256]
        blocks[16 + g] = A[0:128, 128:256]
        blocks[20 + g] = A[128:256, 0:128]
        blocks[24 + g] = A[128:256, 256:384]
        blocks[28 + g] = A[256:384, 128:256]
    return np.ascontiguousarray(blocks.transpose(1, 0, 2))


def _rope_tables(L, identity):
    if identity:
        cos = np.ones((L, 32), np.float32)
        sin = np.zeros((L, 32), np.float32)
    else:
        rows = L // 64
        row = np.repeat(np.arange(rows), 64).astype(np.float64)
        col = np.tile(np.arange(64), rows).astype(np.float64)
        inv = 1.0 / (10000.0 ** (np.arange(0, 32, 2, dtype=np.float64) / 32.0))
        ang = np.concatenate([row[:, None] * inv, col[:, None] * inv], axis=-1)
        cos, sin = np.cos(ang).astype(np.float32), np.sin(ang).astype(np.float32)
    cs = np.zeros((64, 2, L), np.float32)
    cs[0:32, 0] = cos.T
    cs[32:64, 0] = cos.T
    cs[0:32, 1] = -sin.T
    cs[32:64, 1] = sin.T
    return cs


_NC_CACHE = {}


def make_in_maps(x_prompt, x_sample, cache_ckv, cache_krope, c, c_ctx, w_mod, b_mod, g_norm, w_in,
                 g_q, w_uq, g_kv, w_ukv, pool_w, pool_s, conv_w, w_out, g_final):
    f = lambda a: np.ascontiguousarray(np.asarray(a, dtype=np.float32))
    x_prompt, x_sample, cache_ckv, cache_krope, c, c_ctx = map(f, (x_prompt, x_sample, cache_ckv, cache_krope, c, c_ctx))
    w_mod, b_mod, g_norm, w_in, g_q, w_uq, g_kv, w_ukv = map(f, (w_mod, b_mod, g_norm, w_in, g_q, w_uq, g_kv, w_ukv))
    pool_w, pool_s, conv_w, w_out, g_final = map(f, (pool_w, pool_s, conv_w, w_out, g_final))

    def pvec(v):
        v = np.asarray(v)
        lead = v.shape[:-1]
        C = v.shape[-1] // 128
        return np.ascontiguousarray(np.moveaxis(v.reshape(lead + (C, 128)), -1, 0))

    shared = {
        "w_mod": w_mod, "b_modT": pvec(b_mod), "g_normT": pvec(g_norm), "w_in": w_in, "g_qT": pvec(g_q),
        "w_uq": w_uq, "g_kvT": pvec(g_kv), "w_ukv": w_ukv, "pool_w": pool_w, "pool_sT": pvec(pool_s),
        "conv_wT": np.ascontiguousarray(np.transpose(conv_w.reshape(DEPTH, 3, 2, 128), (3, 0, 2, 1))),
        "w_out": w_out, "g_finalT": pvec(g_final),
    }
    blocks_p = _pool_blocks(256)
    blocks_s = _pool_blocks(1024)
    cs_p = _rope_tables(1024, True)
    cs_s = _rope_tables(1024, False)
    in_maps = []
    for core in range(8):
        m = dict(shared)
        if core < 4:
            xs = x_prompt[core * 4:(core + 1) * 4].reshape(T, D)
            m["xT"] = np.ascontiguousarray(xs.T)
            m["cvec"] = pvec(c_ctx)
            m["ctx_ckvT"] = np.zeros((DEPTH, 256, 512), np.float32)
            m["ctx_krT"] = np.zeros((DEPTH, 64, 512), np.float32)
            seq = np.arange(T) // 256
            mq = np.zeros((4, T), np.float32)
            mq[seq, np.arange(T)] = 1.0
            mk = np.full((4, NKEY), -BIG, np.float32)
            for j in range(4):
                mk[j, 512 + np.where(seq == j)[0]] = 0.0
            m["maskq"], m["maskk"] = mq, mk
            m["cs"] = cs_p
            m["poolA"] = blocks_p
            m["convfix"] = np.ones((128, 1), np.float32)
        else:
            b = core - 4
            m["xT"] = np.ascontiguousarray(x_sample[b].T)
            m["cvec"] = pvec(c[b])
            m["ctx_ckvT"] = np.ascontiguousarray(np.transpose(cache_ckv[b], (0, 2, 1)))
            m["ctx_krT"] = np.ascontiguousarray(np.transpose(cache_krope[b], (0, 2, 1)))
            mq = np.zeros((4, T), np.float32)
            mq[0] = 1.0
            m["maskq"], m["maskk"] = mq, np.zeros((4, NKEY), np.float32)
            m["cs"] = cs_s
            m["poolA"] = blocks_s
            m["convfix"] = np.zeros((128, 1), np.float32)
        in_maps.append(m)

    return in_maps


def kernel(**inputs):
    in_maps = make_in_maps(**inputs)
    if "nc" not in _NC_CACHE:
        _NC_CACHE["nc"] = build_program()
    nc = _NC_CACHE["nc"]
    res = run_bass_kernel_spmd(nc, in_maps, core_ids=list(range(8)))
    return assemble(res.results)


def assemble(R):
    y_prompt = np.zeros((16, 256, D), np.float32)
    y_sample = np.zeros((4, 1024, D), np.float32)
    state_ckv = np.zeros((16, DEPTH, 256, 256), np.float32)
    state_krope = np.zeros((16, DEPTH, 256, 64), np.float32)
    for core in range(8):
        yT = np.asarray(R[core]["yT"])
        if core < 4:
            y_prompt[core * 4:(core + 1) * 4] = yT.T.reshape(4, 256, D)
            ck = np.asarray(R[core]["ckvT_out"])
            kr = np.asarray(R[core]["krT_out"])
            state_ckv[core * 4:(core + 1) * 4] = np.transpose(ck.reshape(DEPTH, 256, 4, 256), (2, 0, 3, 1))
            state_krope[core * 4:(core + 1) * 4] = np.transpose(kr.reshape(DEPTH, 64, 4, 256), (2, 0, 3, 1))
        else:
            y_sample[core - 4] = yT.T
    return (y_prompt, y_sample, state_ckv, state_krope)
```

```python
import contextlib
import numpy as np
import concourse.bass as bass
import concourse.mybir as mybir
from concourse.bass_utils import run_bass_kernel_spmd

F32 = mybir.dt.float32
BF = mybir.dt.bfloat16
AF = mybir.ActivationFunctionType
ALU = mybir.AluOpType

D = 1024
T = 1024
NKEY = 1536
DEPTH = 2
IN_W = 2752
ATTN_SCALE = 192 ** -0.5
EPS = 1e-6
BIG = 2000.0
NRING = 3
SAME_ENG_SYNC = True

ENGS = ("pe", "act", "dve", "pool", "sp")


class Op:
    __slots__ = ("eng", "fn", "deps", "signal", "dma_key", "total", "sigval", "idx")


class Sched:
    def __init__(self):
        self.ops = {e: [] for e in ENGS}
        self.lastw = {}
        self.readers = {}
        self.dma_count = {}
        self.out_keys = set()

    def add(self, eng, fn, reads=(), writes=(), dma_key=None, total=False, is_out=False):
        op = Op()
        op.eng, op.fn, op.signal, op.dma_key, op.total = eng, fn, False, dma_key, total
        op.sigval = None
        op.idx = len(self.ops[eng])
        deps = []
        ps_reads = [k for k in reads if k.startswith("ps")]
        if ps_reads:
            reads = [k for k in reads if not k.startswith("ps")]
            writes = list(writes) + ps_reads
        raw = set()
        for k in reads:
            w = self.lastw.get(k)
            if w is not None:
                deps.append(w)
                raw.add(id(w))
        for k in writes:
            w = self.lastw.get(k)
            if w is not None:
                deps.append(w)
            deps.extend(self.readers.get(k, ()))
        for k in reads:
            lst = self.readers.setdefault(k, [])
            if op.dma_key is None:
                lst[:] = [r for r in lst if not (r.dma_key is None and r.eng == eng)]
            lst.append(op)
        for k in writes:
            self.lastw[k] = op
            self.readers[k] = []
        final = []
        seen = set()
        for d in deps:
            if id(d) in seen or d is op:
                continue
            seen.add(id(d))
            if d.dma_key is None and d.eng == eng and op.dma_key is None:
                if eng == "pe" or not SAME_ENG_SYNC or id(d) not in raw:
                    continue
            if d.dma_key is not None and d.dma_key == dma_key:
                continue
            final.append((d, 16 * self.dma_count[d.dma_key] if d.dma_key is not None else None))
            d.signal = True
        op.deps = final
        if dma_key is not None:
            self.dma_count[dma_key] = self.dma_count.get(dma_key, 0) + 1
            op.sigval = 16 * self.dma_count[dma_key]
            if is_out:
                self.out_keys.add(dma_key)
        self.ops[eng].append(op)
        return op

    def emit(self, nc, stack):
        sems = {}
        for e in ENGS:
            sems[e] = stack.enter_context(nc.semaphore("s_" + e))
        for k in self.dma_count:
            sems["dma:" + k] = stack.enter_context(nc.semaphore("d_" + k))
        for e in ENGS:
            c = 0
            for op in self.ops[e]:
                if op.dma_key is None and op.signal:
                    c += 1
                    op.sigval = c
        block = stack.enter_context(nc.Block())
        sched = self

        def run(eng_name, engine):
            waited = {}
            for op in sched.ops[eng_name]:
                need = {}
                for d, dv in op.deps:
                    if d.dma_key is not None:
                        s = "dma:" + d.dma_key
                        v = 16 * sched.dma_count[d.dma_key] if d.total else dv
                    else:
                        s = d.eng
                        v = d.sigval
                    if need.get(s, 0) < v:
                        need[s] = v
                for s, v in need.items():
                    if waited.get(s, 0) < v:
                        engine.wait_ge(sems[s], v)
                        waited[s] = v
                ins = op.fn(engine)
                if op.dma_key is not None:
                    ins.then_inc(sems["dma:" + op.dma_key], 16)
                elif op.signal:
                    ins.then_inc(sems[eng_name], 1)
            if eng_name == "sp":
                for k in sorted(sched.out_keys):
                    engine.wait_ge(sems["dma:" + k], 16 * sched.dma_count[k])

        @block.tensor
        def _(e):
            run("pe", e)

        @block.scalar
        def _(e):
            run("act", e)

        @block.vector
        def _(e):
            run("dve", e)

        @block.gpsimd
        def _(e):
            run("pool", e)

        @block.sync
        def _(e):
            run("sp", e)


WIN_LOADS = [
    (0, 512, [("cq", 0, 0, 128), ("cq", 1, 128, 128), ("cq", 2, 256, 128), ("ckv", 0, 384, 128)]),
    (512, 448, [("ckv", 1, 512, 128), ("kr", 0, 640, 64), ("gm", 0, 704, 128), ("gm", 1, 832, 128)]),
    (960, 512, [("gm", 2, 960, 128), ("gm", 3, 1088, 128), ("px", 0, 1216, 256)]),
    (1472, 512, [("gp", 0, 1472, 128), ("gp", 1, 1600, 128), ("cb", 0, 1728, 128), ("cb", 1, 1856, 128)]),
    (1984, 512, [("cc", 0, 1984, 128), ("ch", 0, 2240, 128), ("cc", 1, 2112, 128), ("ch", 1, 2368, 128)]),
    (2496, 256, [("gc", 0, 2496, 128), ("gc", 1, 2624, 128)]),
]


def build_program():
    nc = bass.Bass("TRN2", target_bir_lowering=False)
    S = Sched()
    dr = {}

    def din(name, shape):
        dr[name] = nc.dram_tensor(name, list(shape), F32, kind="ExternalInput").ap()

    def dout(name, shape):
        dr[name] = nc.dram_tensor(name, list(shape), F32, kind="ExternalOutput").ap()

    din("xT", (D, T)); din("cvec", (128, 8)); din("ctx_ckvT", (DEPTH, 256, 512)); din("ctx_krT", (DEPTH, 64, 512))
    din("maskq", (4, T)); din("maskk", (4, NKEY)); din("cs", (64, 2, T)); din("poolA", (128, 32, 128))
    din("convfix", (128, 1))
    din("w_mod", (DEPTH, D, 3 * D)); din("b_modT", (128, DEPTH, 24)); din("g_normT", (128, DEPTH, 8))
    din("w_in", (DEPTH, D, IN_W)); din("g_qT", (128, DEPTH, 3)); din("w_uq", (DEPTH, 384, 768))
    din("g_kvT", (128, DEPTH, 2)); din("w_ukv", (DEPTH, 256, 1024)); din("pool_w", (DEPTH, 4, 64, 64))
    din("pool_sT", (128, DEPTH, 2)); din("conv_wT", (128, DEPTH, 2, 3)); din("w_out", (DEPTH, D, D))
    din("g_finalT", (128, 8))
    dout("yT", (D, T)); dout("ckvT_out", (DEPTH, 256, T)); dout("krT_out", (DEPTH, 64, T))

    stack = contextlib.ExitStack()
    with stack:
        def sb(name, shape, dt):
            return stack.enter_context(nc.sbuf_tensor(name, list(shape), dt))

        xT = sb("xT_s", (128, 8, T), F32)
        hT = sb("hT", (128, 8, T), BF)
        mixT = sb("mixT", (128, 8, T), BF)
        rs = [sb("rs%d" % i, (128, T), F32) for i in range(2)]
        ring = [sb("ring%d" % i, (128, 4096), BF) for i in range(NRING)]
        R3 = sb("R3", (128, 3072), F32)
        sqs = sb("sqs", (128, 5, T), BF)
        ckvT = sb("ckvT", (128, 2, T), F32)
        krT = sb("krT", (64, T), F32)
        pxtok = sb("pxtok", (128, 2048), BF)
        qrT = [sb("qrT%d" % i, (128, T), BF) for i in range(2)]
        ckvall = sb("ckvall", (128, 2, NKEY), BF)
        krall = [sb("krall%d" % i, (128, NKEY), BF) for i in range(DEPTH)]
        poolA = sb("poolA_s", (128, 32, 128), BF)
        pwpad = [sb("pwpad%d" % i, (64, 4, 128), BF) for i in range(DEPTH)]
        cs = sb("cs_s", (64, 2, T), F32)
        ropeA = sb("ropeA", (64, 512), F32)
        ropeS = sb("ropeS", (64, 512), F32)
        ones = sb("ones", (128, 128), BF)
        epsT = sb("epsT", (128, 1), F32)
        cvec = sb("cvec_s", (128, 8), F32)
        scb = sb("scb", (128, 8), BF)
        b_modT = sb("b_modT_s", (128, DEPTH, 24), F32)
        g_normT = sb("g_normT_s", (128, DEPTH, 8), F32)
        g_qT = sb("g_qT_s", (128, DEPTH, 3), F32)
        g_kvT = sb("g_kvT_s", (128, DEPTH, 2), F32)
        pool_sT = sb("pool_sT_s", (128, DEPTH, 2), F32)
        conv_wT = sb("conv_wT_s", (128, DEPTH, 2, 3), F32)
        g_finalT = sb("g_finalT_s", (128, 8), F32)
        convfix = sb("convfix_s", (128, 1), F32)
        nwb = sb("nwb", (128, DEPTH, 2, 2), F32)
        modT = sb("modT", (128, DEPTH, 24), F32)
        gs = sb("gs", (128, DEPTH, 8), F32)

        hflat = hT[:, :, :].rearrange("p a b -> p (a b)")
        vtok = hflat[:, 0:6144]
        pT = hflat[:, 6144:8192]
        cbT = sb("cbT", (128, 2048), F32)
        zT = sb("zT", (128, 2052), F32)
        knT = [sb("knT%d" % i, (128, NKEY), BF) for i in range(2)]
        qnT = [sb("qnT%d" % i, (128, T), BF) for i in range(2)]
        pooledT = sb("pooledT", (64, 4, 512), BF)
        ps2 = sb("ps2", (128, 3, 512), BF)

        def vkey(kt):
            return "hT%d_%d" % (kt // 2, kt % 2)

        def pkey(slot):
            return "hT%d_%d" % (6 + slot // 2, slot % 2)

        def r3(i):
            return R3[:, i * 1024:(i + 1) * 1024]

        psS = [stack.enter_context(nc.psum_tensor("psS%d" % i, [128, 1024], F32)) for i in range(2)]
        psB = stack.enter_context(nc.psum_tensor("psB", [128, 512], F32))
        psG = [psS[0][:, 0:512], psS[0][:, 512:1024], psS[1][:, 0:512], psS[1][:, 512:1024], psB[:, :]]
        BG_BANK = 4
        psO = [stack.enter_context(nc.psum_tensor("psO%d" % i, [128, 512], F32)) for i in range(2)]
        psD = stack.enter_context(nc.psum_tensor("psD", [128, 512], F32))

        st = {"g": 0, "ring": 0, "ring_order": [0, 1, 2]}

        def bank_general():
            i = st["g"] % 5
            st["g"] += 1
            return i

        def hsl(half):
            return slice(half * 512, (half + 1) * 512)

        small = [(cvec, "cvec"), (b_modT, "b_modT"), (g_normT, "g_normT"), (g_qT, "g_qT"), (g_kvT, "g_kvT"),
                 (pool_sT, "pool_sT"), (conv_wT, "conv_wT"), (g_finalT, "g_finalT"), (convfix, "convfix")]
        for t_, n_ in small[:3]:
            S.add("sp", lambda e, t_=t_, n_=n_: e.dma_start(out=t_[:], in_=dr[n_][:]),
                  writes=["c_" + n_], dma_key="const", total=True)
        for c in range(8):
            S.add("sp", lambda e, c=c: e.dma_start(out=xT[:, c, :], in_=dr["xT"][c * 128:(c + 1) * 128, :]),
                  writes=["xT%d_0" % c, "xT%d_1" % c], dma_key="x%d" % c)
        for t_, n_ in small[3:]:
            S.add("sp", lambda e, t_=t_, n_=n_: e.dma_start(out=t_[:], in_=dr[n_][:]),
                  writes=["c_" + n_], dma_key="const2", total=True)
        S.add("sp", lambda e: e.dma_start(out=cs[:], in_=dr["cs"][:]), writes=["c_cs"], dma_key="const2", total=True)
        S.add("dve", lambda e: e.memset(ones[:], 1.0), writes=["ones"])
        S.add("dve", lambda e: e.memset(epsT[:], EPS), writes=["eps"])
        for l in range(DEPTH):
            S.add("dve", lambda e, l=l: e.memset(pwpad[l][:], 0.0), writes=["pwpad%d_%d" % (l, g) for g in range(4)])
        S.add("dve", lambda e: e.memset(zT[:, :], 0.0), writes=["z0", "z1"])

        def ring_load(dmas, slot=None):
            if slot is None:
                s = st["ring_order"][st["ring"] % NRING]
                st["ring"] += 1
            else:
                s = slot
            key = "ring%d" % s
            for fn in dmas:
                S.add("pool", lambda e, fn=fn, s=s: fn(e, ring[s]), writes=[key], dma_key=key)
            return s, key

        def load_cols(src3, w):
            K = src3.shape[1]
            return [lambda e, slot, src3=src3, K=K, w=w: e.dma_start(
                out=slot[:, 0:K * w].rearrange("p (k n) -> p k n", k=K), in_=src3)]

        S.add("act", lambda e: e.activation(out=scb[:], in_=cvec[:], func=AF.Silu), reads=["c_cvec"], writes=["scb"])

        def mod_load(l, j, slot=None):
            src = dr["w_mod"][l].rearrange("(k p) n -> p k n", p=128)[:, :, j * 512:(j + 1) * 512]
            return ring_load(load_cols(src, 512), slot)

        def mod_part(l, j, bank=None, slot=None, loaded=None):
            s, key = loaded if loaded is not None else mod_load(l, j, slot)
            a = bank_general() if bank is None else bank
            pk = "psG%d" % a
            for m in range(4):
                for k in range(8):
                    S.add("pe", lambda e, s=s, m=m, k=k, a=a: e.matmul(
                        psG[a][:, m:m + 1], ring[s][:, k * 512 + m * 128:k * 512 + (m + 1) * 128],
                        scb[:, k:k + 1], start=(k == 0), stop=(k == 7)),
                        reads=[key, "scb"], writes=[pk])
            S.add("dve", lambda e, a=a: e.tensor_tensor(out=modT[:, l, j * 4:j * 4 + 4], in0=psG[a][:, 0:4],
                                                       in1=b_modT[:, l, j * 4:j * 4 + 4], op=ALU.add),
                  reads=[pk, "c_b_modT"], writes=["modT%d_%d" % (l, j)])
            if j == 3:
                S.add("dve", lambda e: e.scalar_tensor_tensor(out=gs[:, l, :], in0=modT[:, l, 8:16], scalar=1.0,
                                                              in1=g_normT[:, l, :], op0=ALU.add, op1=ALU.mult),
                      reads=["modT%d_2" % l, "modT%d_3" % l, "c_g_normT"], writes=["gs%d" % l])

        for l_ in range(DEPTH):
            for i in range(2):
                for j, kk in enumerate((0, 2)):
                    S.add("dve", lambda e, l_=l_, i=i, j=j, kk=kk: e.tensor_scalar(
                        out=nwb[:, l_, i, j:j + 1], in0=conv_wT[:, l_, i, kk:kk + 1], scalar1=convfix[:, 0:1],
                        scalar2=-1.0, op0=ALU.mult, op1=ALU.mult),
                        reads=["c_conv_wT", "c_convfix"], writes=["nwb"])

        def rstd_from(sq_aps_keys, half, nfeat, rs_t, rs_key):
            n = len(sq_aps_keys)
            a = bank_general()
            pk = "psG%d" % a
            for i, (ap, key) in enumerate(sq_aps_keys):
                S.add("pe", lambda e, ap=ap, i=i, n=n, a=a: e.matmul(psG[a][:, :], ones[:, :], ap, start=(i == 0),
                                                                    stop=(i == n - 1)),
                      reads=[key, "ones"], writes=[pk])
            S.add("act", lambda e, a=a: e.activation(out=rs_t[:, hsl(half)], in_=psG[a][:, :], func=AF.Ln,
                                                     scale=1.0 / nfeat, bias=epsT[:, 0:1]),
                  reads=[pk, "eps"], writes=[rs_key + str(half)])
            S.add("act", lambda e: e.activation(out=rs_t[:, hsl(half)], in_=rs_t[:, hsl(half)], func=AF.Exp,
                                                scale=-0.5),
                  reads=[rs_key + str(half)], writes=[rs_key + str(half)])

        def rope(ps_ap, ps_key, half, out_ap, out_key, copy_eng="dve"):
            if copy_eng == "dve":
                S.add("dve", lambda e: e.tensor_copy(out=ropeA[:, :], in_=ps_ap[0:64, :]),
                      reads=[ps_key], writes=["ropeA"])
            else:
                S.add("act", lambda e: e.activation(out=ropeA[:, :], in_=ps_ap[0:64, :], func=AF.Copy),
                      reads=[ps_key], writes=["ropeA"])
            S.add("dve", lambda e: e.tensor_copy(out=ropeS[0:32, :], in_=ropeA[32:64, :]),
                  reads=["ropeA"], writes=["ropeS"])
            S.add("dve", lambda e: e.tensor_copy(out=ropeS[32:64, :], in_=ropeA[0:32, :]),
                  reads=["ropeA"], writes=["ropeS"])
            S.add("dve", lambda e: e.tensor_tensor(out=ropeS[:, :], in0=ropeS[:, :], in1=cs[:, 1, hsl(half)],
                                                   op=ALU.mult),
                  reads=["ropeS", "c_cs"], writes=["ropeS"])
            S.add("dve", lambda e: e.tensor_tensor(out=ropeA[:, :], in0=ropeA[:, :], in1=cs[:, 0, hsl(half)],
                                                   op=ALU.mult),
                  reads=["ropeA", "c_cs"], writes=["ropeA"])
            S.add("dve", lambda e: e.tensor_tensor(out=out_ap, in0=ropeA[:, :], in1=ropeS[:, :], op=ALU.add),
                  reads=["ropeA", "ropeS"], writes=[out_key])

        def pool_consts():
            for hh in range(2):
                S.add("pool", lambda e, hh=hh: e.dma_start(out=qrT[hh][64:68, :], in_=dr["maskq"][:, :]),
                      writes=["qrm%d" % hh], dma_key="constp", total=True)
            for l_ in range(DEPTH):
                S.add("pool", lambda e, l_=l_: e.dma_start(out=krall[l_][64:68, :], in_=dr["maskk"][:, :]),
                      writes=["krm%d" % l_], dma_key="constp", total=True)
            S.add("pool", lambda e: e.dma_start(out=poolA[:], in_=dr["poolA"][:]), writes=["poolA"], dma_key="constp",
                  total=True)
            for ll in range(DEPTH):
                for g in range(4):
                    S.add("pool", lambda e, ll=ll, g=g: e.dma_start(
                        out=pwpad[ll][0:64, g, (g % 2) * 64:(g % 2) * 64 + 64], in_=dr["pool_w"][ll, g, :, :]),
                        writes=["pwpad%d_%d" % (ll, g)], dma_key="constp", total=True)


        def evac(l, kind, idx, half, ps, pk):
            h_ = hsl(half)
            if kind == "cq":
                S.add("dve", lambda e: e.tensor_copy(out=r3(idx)[:, h_], in_=ps[:, :]),
                      reads=[pk], writes=["R3_%d_%d" % (idx, half)])
                S.add("act", lambda e: e.activation(out=sqs[:, idx, h_], in_=r3(idx)[:, h_], func=AF.Square),
                      reads=["R3_%d_%d" % (idx, half)], writes=["sqs%d_%d" % (idx, half)])
            elif kind == "ckv":
                S.add("dve", lambda e: e.tensor_copy(out=ckvT[:, idx, h_], in_=ps[:, :]),
                      reads=[pk], writes=["ckvT%d_%d" % (idx, half)])
                S.add("act", lambda e: e.activation(out=sqs[:, 3 + idx, h_], in_=ckvT[:, idx, h_], func=AF.Square),
                      reads=["ckvT%d_%d" % (idx, half)], writes=["sqs%d_%d" % (3 + idx, half)])
            elif kind == "kr":
                rope(ps, pk, half, krT[:, h_], "krT%d" % half)
                S.add("dve", lambda e: e.tensor_copy(out=krall[l][0:64, 512 + half * 512: 1024 + half * 512],
                                                     in_=krT[:, h_]),
                      reads=["krT%d" % half], writes=["krall_cur%d_%d" % (l, half)])
            elif kind in ("gm", "gp", "gc"):
                ch_ = {"gm": 0, "gp": 4, "gc": 6}[kind] + idx
                S.add("act", lambda e: e.activation(out=mixT[:, ch_, h_], in_=ps[:, :], func=AF.Silu),
                      reads=[pk], writes=["mix%d_%d" % (ch_, half)])
            elif kind == "cb":
                S.add("dve", lambda e: e.tensor_copy(
                    out=cbT[:, idx * 1024 + half * 512: idx * 1024 + half * 512 + 512], in_=ps[:, :]),
                    reads=[pk], writes=["cb%d" % idx])
            elif kind == "cc":
                S.add("act", lambda e: e.activation(out=r3(1)[:, h_], in_=ps[:, :], func=AF.Copy),
                      reads=[pk], writes=["R3_1_%d" % half])
            elif kind == "ch":
                S.add("dve", lambda e: e.tensor_tensor(
                    out=zT[:, idx * 1026 + 1 + half * 512: idx * 1026 + 1 + half * 512 + 512], in0=ps[:, :],
                    in1=r3(1)[:, h_], op=ALU.mult),
                    reads=[pk, "R3_1_%d" % half], writes=["z%d" % idx])

        def xnorm_stats(l):
            for half in range(2):
                xnorm_stats_half(l, half)

        def xnorm_stats_half(l, half):
            for _ in range(1):
                for c in range(8):
                    S.add("act", lambda e, c=c, half=half: e.activation(out=mixT[:, c, hsl(half)],
                                                                        in_=xT[:, c, hsl(half)], func=AF.Square),
                          reads=["xT%d_%d" % (c, half)], writes=["mix%d_%d" % (c, half)])
                rstd_from([(mixT[:, c, hsl(half)], "mix%d_%d" % (c, half)) for c in range(8)], half, D, rs[0], "rsx")
                if l == 0:
                    for c in range(8):
                        S.add("dve", lambda e, c=c, half=half: e.tensor_tensor(
                            out=hT[:, c, hsl(half)], in0=xT[:, c, hsl(half)], in1=rs[0][:, hsl(half)], op=ALU.mult),
                            reads=["xT%d_%d" % (c, half), "rsx%d" % half], writes=["hT%d_%d" % (c, half)])

        def xnorm_apply(l, halves=(0, 1)):
            for half in halves:
                for c in range(8):
                    xnorm_apply_one(l, half, c)

        def xnorm_apply_one(l, half, c):
            for _ in range(1):
                for __ in range(1):
                    hk = "hT%d_%d" % (c, half)
                    mk = ["gs%d" % l, "modT%d_%d" % (l, c // 4)]
                    if l == 0:
                        if c % 4 == 0:
                            S.add("act", lambda e, c=c, half=half: e.activation(
                                out=hT[:, c, hsl(half)], in_=hT[:, c, hsl(half)], func=AF.Identity,
                                scale=gs[:, l, c:c + 1], bias=modT[:, l, c:c + 1]),
                                reads=[hk] + mk, writes=[hk])
                        else:
                            S.add("dve", lambda e, c=c, half=half: e.tensor_scalar(
                                out=hT[:, c, hsl(half)], in0=hT[:, c, hsl(half)], scalar1=gs[:, l, c:c + 1],
                                scalar2=modT[:, l, c:c + 1], op0=ALU.mult, op1=ALU.add),
                                reads=[hk] + mk, writes=[hk])
                    else:
                        par = c % 2
                        tb = r3(par)[:, hsl(half)]
                        tk = "R3_%d_%d" % (par, half)
                        S.add("dve", lambda e, c=c, half=half, tb=tb: e.tensor_tensor(
                            out=tb, in0=xT[:, c, hsl(half)], in1=rs[0][:, hsl(half)], op=ALU.mult),
                            reads=["xT%d_%d" % (c, half), "rsx%d" % half], writes=[tk])
                        S.add("act", lambda e, c=c, half=half, tb=tb: e.activation(
                            out=hT[:, c, hsl(half)], in_=tb, func=AF.Identity, scale=gs[:, l, c:c + 1],
                            bias=modT[:, l, c:c + 1]),
                            reads=[tk] + mk, writes=[hk])


        def final_stats_half(half):
            for c in range(8):
                S.add("act", lambda e, c=c: e.activation(out=mixT[:, c, hsl(half)], in_=xT[:, c, hsl(half)],
                                                         func=AF.Square),
                      reads=["xT%d_%d" % (c, half)], writes=["mix%d_%d" % (c, half)])
            rstd_from([(mixT[:, c, hsl(half)], "mix%d_%d" % (c, half)) for c in range(8)], half, D, rs[0], "rsx")

        def final_apply_one(half, c):
            S.add("dve", lambda e: e.scalar_tensor_tensor(
                out=xT[:, c, hsl(half)], in0=xT[:, c, hsl(half)], scalar=g_finalT[:, c:c + 1],
                in1=rs[0][:, hsl(half)], op0=ALU.mult, op1=ALU.mult),
                reads=["xT%d_%d" % (c, half), "rsx%d" % half, "c_g_finalT"], writes=["xT%d_%d" % (c, half)])
            S.add("sp", lambda e: e.dma_start(out=dr["yT"][c * 128:(c + 1) * 128, hsl(half)], in_=xT[:, c, hsl(half)]),
                  reads=["xT%d_%d" % (c, half)], dma_key="yout", total=True, is_out=True)

        preloaded = {}

        def layer(l):
            def ctx_loads():
                for k in range(2):
                    S.add("pool", lambda e, k=k: e.dma_start(out=ckvall[:, k, 0:512],
                                                              in_=dr["ctx_ckvT"][l, k * 128:(k + 1) * 128, :]),
                          writes=["ckvall_ctx"], dma_key="ctxc%d" % l)
                S.add("pool", lambda e: e.dma_start(out=krall[l][0:64, 0:512], in_=dr["ctx_krT"][l, :, :]),
                      writes=["krall_ctx%d" % l], dma_key="ctxk%d" % l)

            if l > 0:
                ctx_loads()

            if l > 0:
                xnorm_stats_half(l, 1)
                xnorm_apply(l, halves=(1,))
            else:
                xnorm_apply(l)

            def norms():
                for half in range(2):
                    rstd_from([(sqs[:, i, hsl(half)], "sqs%d_%d" % (i, half)) for i in range(3)], half, 384,
                              rs[1], "rsq")
                for i in range(3):
                    S.add("dve", lambda e, i=i: e.scalar_tensor_tensor(
                        out=sqs[:, i, :], in0=r3(i), scalar=g_qT[:, l, i:i + 1], in1=rs[1][:, :],
                        op0=ALU.mult, op1=ALU.mult),
                        reads=["R3_%d_0" % i, "R3_%d_1" % i, "rsq0", "rsq1", "c_g_qT"],
                        writes=["sqs%d_0" % i, "sqs%d_1" % i])
                for half in range(2):
                    rstd_from([(sqs[:, 3 + i, hsl(half)], "sqs%d_%d" % (3 + i, half)) for i in range(2)], half,
                              256, rs[0], "rsx")
                for i in range(2):
                    S.add("dve", lambda e, i=i: e.scalar_tensor_tensor(
                        out=ckvT[:, i, :], in0=ckvT[:, i, :], scalar=g_kvT[:, l, i:i + 1], in1=rs[0][:, :],
                        op0=ALU.mult, op1=ALU.mult),
                        reads=["ckvT%d_0" % i, "ckvT%d_1" % i, "rsx0", "rsx1", "c_g_kvT"],
                        writes=["ckvT%d_0" % i, "ckvT%d_1" % i])
                    S.add("dve", lambda e, i=i: e.tensor_copy(out=ckvall[:, i, 512:NKEY], in_=ckvT[:, i, :]),
                          reads=["ckvT%d_0" % i, "ckvT%d_1" % i], writes=["ckvall_cur%d" % i])
                    S.add("sp", lambda e, i=i: e.dma_start(out=dr["ckvT_out"][l, i * 128:(i + 1) * 128, :],
                                                           in_=ckvT[:, i, :]),
                          reads=["ckvT%d_0" % i, "ckvT%d_1" % i], dma_key="ckvo%d" % l, is_out=True)
                S.add("sp", lambda e: e.dma_start(out=dr["krT_out"][l, :, :], in_=krT[:, :]),
                      reads=["krT0", "krT1"], dma_key="kro%d" % l, is_out=True)

            conv_ops = []
            _S_add = S.add

            def _defer(*a, **k):
                conv_ops.append(lambda: _S_add(*a, **k))

            def conv_chunk(i):
                acc = r3(0) if i == 0 else r3(2)
                akey = ["R3_%d_0" % (0 if i == 0 else 2), "R3_%d_1" % (0 if i == 0 else 2)]
                zb = i * 1026
                _defer("dve", lambda e, i=i, zb=zb: e.tensor_scalar(
                    out=acc, in0=zT[:, zb + 1:zb + 1025], scalar1=conv_wT[:, l, i, 1:2], scalar2=None,
                    op0=ALU.mult),
                    reads=["z%d" % i, "c_conv_wT"], writes=akey)
                _defer("dve", lambda e, i=i, zb=zb: e.scalar_tensor_tensor(
                    out=acc, in0=zT[:, zb:zb + 1024], scalar=conv_wT[:, l, i, 0:1], in1=acc,
                    op0=ALU.mult, op1=ALU.add),
                    reads=["z%d" % i] + akey, writes=akey)
                _defer("dve", lambda e, i=i, zb=zb: e.scalar_tensor_tensor(
                    out=acc, in0=zT[:, zb + 2:zb + 1026], scalar=conv_wT[:, l, i, 2:3], in1=acc,
                    op0=ALU.mult, op1=ALU.add),
                    reads=["z%d" % i] + akey, writes=akey)
                _defer("dve", lambda e, i=i, zb=zb: e.scalar_tensor_tensor(
                    out=acc[:, 256:1024:256], in0=zT[:, zb + 256:zb + 1024:256], scalar=nwb[:, l, i, 0:1],
                    in1=acc[:, 256:1024:256], op0=ALU.mult, op1=ALU.add),
                    reads=["z%d" % i, "nwb"] + akey, writes=akey)
                _defer("dve", lambda e, i=i, zb=zb: e.scalar_tensor_tensor(
                    out=acc[:, 255:1023:256], in0=zT[:, zb + 257:zb + 1025:256], scalar=nwb[:, l, i, 1:2],
                    in1=acc[:, 255:1023:256], op0=ALU.mult, op1=ALU.add),
                    reads=["z%d" % i, "nwb"] + akey, writes=akey)
                _defer("dve", lambda e, i=i: e.tensor_tensor(out=acc, in0=acc,
                                                            in1=cbT[:, i * 1024:(i + 1) * 1024], op=ALU.mult),
                      reads=akey + ["cb%d" % i], writes=akey)
                _defer("dve", lambda e, i=i: e.tensor_tensor(out=mixT[:, 6 + i, :], in0=acc, in1=mixT[:, 6 + i, :],
                                                            op=ALU.mult),
                      reads=akey + ["mix%d_0" % (6 + i), "mix%d_1" % (6 + i)],
                      writes=["mix%d_0" % (6 + i), "mix%d_1" % (6 + i)])

            for i_ in range(2):
                conv_chunk(i_)
            conv_gate = [conv_ops.pop(6), conv_ops.pop(12)]
            conv_avail = {"n": 0}

            for li, (c0, w, chunks) in enumerate(WIN_LOADS):
                src = dr["w_in"][l].rearrange("(k p) n -> p k n", p=128)[:, :, c0:c0 + w]
                if (l, li) in preloaded:
                    s, key = preloaded[(l, li)]
                    st["ring"] += 1
                else:
                    s, key = ring_load(load_cols(src, w))
                if li == 0:
                    items = [(ch_, hf_) for hf_ in range(2) for ch_ in chunks]
                else:
                    items = [(ch_, hf_) for ch_ in chunks for hf_ in ((None,) if ch_[0] == "px" else (0, 1))]
                for ((kind, idx, col, cw), half_sel) in items:
                    off = col - c0
                    if kind == "px":
                        for jp in range(4):
                            a = bank_general()
                            pk = "psG%d" % a
                            for jj in range(2):
                                j = jp * 2 + jj
                                for k in range(8):
                                    S.add("pe", lambda e, a=a, jj=jj, j=j, k=k, s=s, off=off, w=w: e.matmul(
                                        psG[a][:, jj * 256:(jj + 1) * 256], hT[:, k, j * 128:(j + 1) * 128],
                                        ring[s][:, k * w + off:k * w + off + 256], start=(k == 0), stop=(k == 7)),
                                        reads=[key, "hT%d_%d" % (k, j // 4)], writes=[pk])
                            S.add("dve", lambda e, a=a, jp=jp: e.tensor_copy(
                                out=pxtok[:, jp * 512:(jp + 1) * 512], in_=psG[a][:, :]),
                                reads=[pk], writes=["pxtok%d" % jp])
                        continue
                    for half in (half_sel,):
                        a = bank_general()
                        pk = "psG%d" % a
                        for k in range(8):
                            S.add("pe", lambda e, a=a, k=k, s=s, off=off, w=w, cw=cw, half=half: e.matmul(
                                psG[a][0:cw, :], ring[s][:, k * w + off:k * w + off + cw], hT[:, k, hsl(half)],
                                start=(k == 0), stop=(k == 7)),
                                reads=[key, "hT%d_%d" % (k, half)], writes=[pk])
                        evac(l, kind, idx, half, psG[a], pk)
                        if kind == "ch" and half == 1:
                            conv_avail["n"] += 6
                        for _ in range(2):
                            if conv_avail["n"] > 0 and conv_ops and (kind in ("ch", "gc")):
                                conv_ops.pop(0)()
                                conv_avail["n"] -= 1
                if li == 1 and l == 0:
                    pool_consts()
                    ctx_loads()
                if li == 3:
                    norms()

            while conv_ops:
                conv_ops.pop(0)()
            for f in conv_gate:
                f()

            srcq = dr["w_uq"][l].rearrange("(k p) n -> p k n", p=128)
            sq_, keyq = ring_load(load_cols(srcq, 768))
            srck = dr["w_ukv"][l].rearrange("(k p) (h t d) -> p k t h d", p=128, h=4, t=2)
            s_kv = st["ring_order"][st["ring"] % NRING]
            keykv = "ring%d" % s_kv
            st["ring"] += 1
            for t_ in range(2):
                for k_ in range(2):
                    S.add("pool", lambda e, t_=t_, k_=k_: e.dma_start(
                        out=ring[s_kv][:, 0:2048].rearrange("p (k t h d) -> p k t h d", k=2, t=2, h=4)[:, k_, t_],
                        in_=srck[:, k_, t_]),
                        writes=[keykv], dma_key=keykv)

            def wk_ap(k, h):
                o = k * 1024 + h * 128
                return ring[s_kv][:, o:o + 128]

            def wv_ap(k):
                o = k * 1024 + 512
                return ring[s_kv][:, o:o + 512]

            def v_tile(kt):
                def f(bank=None):
                    a = bank_general() if bank is None else bank
                    pk = "psG%d" % a
                    for k in range(2):
                        S.add("pe", lambda e, k=k: e.matmul(
                            psG[a][:, :], ckvall[:, k, kt * 128:(kt + 1) * 128], wv_ap(k), start=(k == 0),
                            stop=(k == 1)),
                            reads=[keykv, "ckvall_ctx" if kt < 4 else "ckvall_cur%d" % k], writes=[pk])
                    if kt % 2 == 0:
                        S.add("dve", lambda e: e.tensor_copy(out=vtok[:, kt * 512:(kt + 1) * 512], in_=psG[a][:, :]),
                              reads=[pk], writes=[vkey(kt)])
                    else:
                        S.add("act", lambda e: e.activation(out=vtok[:, kt * 512:(kt + 1) * 512], in_=psG[a][:, :],
                                                            func=AF.Copy),
                              reads=[pk], writes=[vkey(kt)])
                return f

            def diag_blk(j, g):
                return (0 if j == 0 else 4 if j == 7 else 8 if j % 2 == 0 else 12) + g

            def pool_tile(jh, g):
                def f(bank=None):
                    a = bank_general() if bank is None else bank
                    pk = "psG%d" % a
                    for jj in range(4):
                        j = jh * 4 + jj
                        terms = [(j, diag_blk(j, g))]
                        if j > 0:
                            terms.append((j - 1, (24 if (j - 1) in (1, 3, 5) else 16) + g))
                        if j < 7:
                            terms.append((j + 1, (28 if j in (1, 3, 5) else 20) + g))
                        for ti, (i, blk) in enumerate(terms):
                            S.add("pe", lambda e, jj=jj, i=i, blk=blk, ti=ti, n=len(terms): e.matmul(
                                psG[a][0:64, jj * 128:(jj + 1) * 128], pxtok[:, i * 256 + g * 64:i * 256 + g * 64 + 64],
                                poolA[:, blk, :], start=(ti == 0), stop=(ti == n - 1)),
                                reads=["pxtok%d" % (i // 2), "poolA"], writes=[pk])
                    S.add("act", lambda e: e.activation(out=pooledT[:, g, :], in_=psG[a][0:64, :], func=AF.Copy),
                          reads=[pk], writes=["pooled%d" % g])
                return f

            def pooly_tile(half, c):
                def f(bank=None):
                    a = bank_general() if bank is None else bank
                    pk = "psG%d" % a
                    for gg in range(2):
                        g = 2 * c + gg
                        S.add("pe", lambda e, g=g, gg=gg: e.matmul(
                            psG[a][:, :], pwpad[l][0:64, g, :], pooledT[:, g, :], start=(gg == 0), stop=(gg == 1)),
                            reads=["pwpad%d_%d" % (l, g), "pooled%d" % g], writes=[pk])
                    S.add("dve", lambda e: e.scalar_tensor_tensor(
                        out=mixT[:, 4 + c, hsl(half)], in0=psG[a][:, :], scalar=pool_sT[:, l, c:c + 1],
                        in1=mixT[:, 4 + c, hsl(half)], op0=ALU.mult, op1=ALU.mult),
                        reads=[pk, "c_pool_sT", "mix%d_%d" % (4 + c, half)], writes=["mix%d_%d" % (4 + c, half)])
                return f

            def prep_tiles(h, fg=False):
                hh = h % 2
                tiles = []

                def qn(half):
                    def f(bank=None):
                        a = bank_general() if bank is None else bank
                        pk = "psG%d" % a
                        for k in range(3):
                            S.add("pe", lambda e, k=k: e.matmul(
                                psG[a][:, :], ring[sq_][:, k * 768 + h * 192:k * 768 + h * 192 + 128],
                                sqs[:, k, hsl(half)], start=(k == 0), stop=(k == 2)),
                                reads=[keyq, "sqs%d_%d" % (k, half)], writes=[pk])
                        if fg:
                            S.add("act", lambda e: e.activation(out=qnT[hh][:, hsl(half)], in_=psG[a][:, :],
                                                                func=AF.Copy),
                                  reads=[pk], writes=["qnT%d_%d" % (hh, half)])
                        else:
                            S.add("dve", lambda e: e.tensor_copy(out=qnT[hh][:, hsl(half)], in_=psG[a][:, :]),
                                  reads=[pk], writes=["qnT%d_%d" % (hh, half)])
                    return f

                def qr(half):
                    def f(bank=None):
                        a = bank_general() if bank is None else bank
                        pk = "psG%d" % a
                        for k in range(3):
                            S.add("pe", lambda e, k=k: e.matmul(
                                psG[a][0:64, :], ring[sq_][:, k * 768 + h * 192 + 128:k * 768 + h * 192 + 192],
                                sqs[:, k, hsl(half)], start=(k == 0), stop=(k == 2)),
                                reads=[keyq, "sqs%d_%d" % (k, half)], writes=[pk])
                        rope(psG[a], pk, half, qrT[hh][0:64, hsl(half)], "qrT%d_%d" % (hh, half),
                             "act" if fg else "dve")
                    return f

                def kn(kb):
                    def f(bank=None):
                        a = bank_general() if bank is None else bank
                        pk = "psG%d" % a
                        for k in range(2):
                            S.add("pe", lambda e, k=k: e.matmul(
                                psG[a][:, :], wk_ap(k, h), ckvall[:, k, kb * 512:(kb + 1) * 512], start=(k == 0),
                                stop=(k == 1)),
                                reads=[keykv, "ckvall_ctx" if kb == 0 else "ckvall_cur%d" % k], writes=[pk])
                        if fg:
                            S.add("act", lambda e: e.activation(out=knT[hh][:, kb * 512:(kb + 1) * 512],
                                                                in_=psG[a][:, :], func=AF.Copy),
                                  reads=[pk], writes=["knT%d_%d" % (hh, kb)])
                        else:
                            S.add("dve", lambda e: e.tensor_copy(out=knT[hh][:, kb * 512:(kb + 1) * 512],
                                                                 in_=psG[a][:, :]),
                                  reads=[pk], writes=["knT%d_%d" % (hh, kb)])
                    return f

                return [qn(0), kn(0), kn(1), qr(0), qn(1), kn(2), qr(1)]

            free_slot = [i for i in range(NRING) if i not in (sq_, s_kv)][0]

            mod_jobs = []
            if l == 0:
                mod_jobs += [(0, 4), (0, 5)]
            if l + 1 < DEPTH:
                mod_jobs += [(l + 1, j) for j in range(6)]
            mod_state = {"next": 0, "loaded": None}

            def mod_prefetch():
                i = mod_state["next"]
                if i < len(mod_jobs):
                    mod_state["loaded"] = mod_load(mod_jobs[i][0], mod_jobs[i][1], free_slot)

            def mod_step(bank):
                i = mod_state["next"]
                if i < len(mod_jobs):
                    mod_part(mod_jobs[i][0], mod_jobs[i][1], bank, free_slot, mod_state["loaded"])
                    mod_state["next"] = i + 1
                    mod_prefetch()

            pending = {"fin": None, "fin_a": None}

            def unit(h, qh, bg):
                hh = h % 2
                u = h * 2 + qh
                ob = u % 2

                def Spair(jp):
                    for t in range(2):
                        kt = 2 * jp + t
                        a = 2 * (jp % 2) + t
                        pk = "psG%d" % a
                        S.add("pe", lambda e, kt=kt, a=a: e.matmul(
                            psG[a][:, :], knT[hh][:, kt * 128:(kt + 1) * 128], qnT[hh][:, hsl(qh)], start=True,
                            stop=False),
                            reads=["knT%d_%d" % (hh, kt // 4), "qnT%d_%d" % (hh, qh)], writes=[pk])
                        S.add("pe", lambda e, kt=kt, a=a: e.matmul(
                            psG[a][:, :], krall[l][0:68, kt * 128:(kt + 1) * 128], qrT[hh][0:68, hsl(qh)],
                            start=False, stop=True),
                            reads=["krall_ctx%d" % l if kt < 4 else "krall_cur%d_%d" % (l, (kt - 4) // 4),
                                   "krm%d" % l, "qrT%d_%d" % (hh, qh), "qrm%d" % hh], writes=[pk])

                def Epair(jp):
                    p = jp % 2
                    S.add("act", lambda e: e.activation(
                        out=pT[:, p * 1024:(p + 1) * 1024], in_=psS[p][:, :], func=AF.Exp, scale=ATTN_SCALE),
                        reads=["psG%d" % (2 * p), "psG%d" % (2 * p + 1)], writes=[pkey(2 * p), pkey(2 * p + 1)])

                def Vpair(jp):
                    for t in range(2):
                        kt = 2 * jp + t
                        slot = 2 * (jp % 2) + t
                        S.add("pe", lambda e, kt=kt, slot=slot: e.matmul(
                            psO[ob][:, :], vtok[:, kt * 512 + h * 128:kt * 512 + (h + 1) * 128],
                            pT[:, slot * 512:(slot + 1) * 512], start=(kt == 0), stop=(kt == 11)),
                            reads=[vkey(kt), pkey(slot)], writes=["psO%d" % ob])

                def PSop(jp):
                    s0 = 2 * (jp % 2)
                    S.add("pool", lambda e: e.tensor_tensor(
                        out=ps2[:, jp % 3, :], in0=pT[:, s0 * 512:(s0 + 1) * 512],
                        in1=pT[:, (s0 + 1) * 512:(s0 + 2) * 512], op=ALU.add),
                        reads=[pkey(s0), pkey(s0 + 1)], writes=["ps2_%d" % (jp % 3)])

                def Dop(jp):
                    S.add("pe", lambda e: e.matmul(
                        psD[:, :], ones[:, :], ps2[:, jp % 3, :], start=(jp == 0), stop=(jp == 5)),
                        reads=["ones", "ps2_%d" % (jp % 3)], writes=["psD"])

                rd = r3(2)[:, ob * 512:(ob + 1) * 512]
                at = r3(0)[:, ob * 512:(ob + 1) * 512]
                rk = "R3_2_%d" % ob

                def finalize_a():
                    Dop(5)
                    S.add("act", lambda e: e.activation(out=rd, in_=psD[:, :], func=AF.Ln),
                          reads=["psD"], writes=[rk])

                def finalize():
                    S.add("act", lambda e: e.activation(out=rd, in_=rd, func=AF.Exp, scale=-1.0),
                          reads=[rk], writes=[rk])
                    S.add("dve", lambda e: e.tensor_tensor(out=at, in0=psO[ob][:, :], in1=rd, op=ALU.mult),
                          reads=["psO%d" % ob, rk], writes=["R3_0_%d" % ob])
                    S.add("dve", lambda e: e.tensor_tensor(out=mixT[:, h, hsl(qh)], in0=at, in1=mixT[:, h, hsl(qh)],
                                                           op=ALU.mult),
                          reads=["R3_0_%d" % ob, "mix%d_%d" % (h, qh)], writes=["mix%d_%d" % (h, qh)])

                Spair(0)
                Spair(1)
                for jp in range(6):
                    Epair(jp)
                    PSop(jp)
                    if jp == 0 and pending["fin_a"] is not None:
                        pending["fin_a"]()
                        pending["fin_a"] = None
                    if jp == 1 and pending["fin"] is not None:
                        pending["fin"]()
                        pending["fin"] = None
                    if jp + 2 < 6:
                        Spair(jp + 2)
                    Vpair(jp)
                    if jp >= 1:
                        Dop(jp - 1)
                    if jp < 4 and bg:
                        bg.pop(0)(BG_BANK)
                    if jp == 4:
                        mod_step(BG_BANK)
                pending["fin_a"] = finalize_a
                pending["fin"] = finalize

            fg = [v_tile(kt) for kt in range(12)]
            p0 = prep_tiles(0, fg=True)
            pl = []
            for jh in range(2):
                pl += [pool_tile(jh, g) for g in range(4)] + [pooly_tile(jh, c) for c in range(2)]
            pq = [p0[i] for i in (0, 3, 4, 6)]
            pk_ = [p0[i] for i in (1, 2, 5)]
            order = []
            while pl or pq:
                if pl:
                    order.append(pl.pop(0))
                if pq:
                    order.append(pq.pop(0))
                if pl:
                    order.append(pl.pop(0))
            while fg or pk_:
                for _ in range(4):
                    if fg:
                        order.append(fg.pop(0))
                if pk_:
                    order.append(pk_.pop(0))
            st["g"] += (4 - (st["g"] + len(order) - 1)) % 5
            for f in order:
                f()

            mod_prefetch()
            bg = prep_tiles(1)
            unit(0, 0, bg)
            unit(0, 1, bg)
            bg += prep_tiles(2)
            unit(1, 0, bg)
            unit(1, 1, bg)
            bg += prep_tiles(3)
            unit(2, 0, bg)
            unit(2, 1, bg)
            assert not bg
            wo = []
            for ob_ in range(2):
                src = dr["w_out"][l].rearrange("(k p) n -> p k n", p=128)[:, :, ob_ * 512:(ob_ + 1) * 512]
                wo.append(ring_load(load_cols(src, 512), (sq_, s_kv)[ob_]))
            unit(3, 0, bg)
            unit(3, 1, bg)
            assert mod_state["next"] == len(mod_jobs)
            if l + 1 < DEPTH:
                c0, w, _ = WIN_LOADS[0]
                src = dr["w_in"][l + 1].rearrange("(k p) n -> p k n", p=128)[:, :, c0:c0 + w]
                preloaded[(l + 1, 0)] = ring_load(load_cols(src, w), free_slot)
            pending["fin_a"]()
            pending["fin"]()
            pending["fin_a"] = None
            pending["fin"] = None

            tix = 0
            for half in range(2):
                for ob_ in range(2):
                    s, key = wo[ob_]
                    for m in range(4):
                        oc = ob_ * 4 + m
                        if half == 1:
                            if tix == 0:
                                if l + 1 < DEPTH:
                                    xnorm_stats_half(l + 1, 0)
                                else:
                                    final_stats_half(0)
                            else:
                                for c_ in ((0, 1) if tix == 1 else (tix,)):
                                    if l + 1 < DEPTH:
                                        xnorm_apply_one(l + 1, 0, c_)
                                    else:
                                        final_apply_one(0, c_)
                            tix += 1
                        a = bank_general()
                        pk = "psG%d" % a
                        for k in range(8):
                            S.add("pe", lambda e, a=a, k=k, s=s, m=m, half=half: e.matmul(
                                psG[a][:, :], ring[s][:, k * 512 + m * 128:k * 512 + (m + 1) * 128],
                                mixT[:, k, hsl(half)], start=(k == 0), stop=(k == 7)),
                                reads=[key, "mix%d_%d" % (k, half)], writes=[pk])
                        S.add("dve", lambda e, a=a, oc=oc, half=half: e.scalar_tensor_tensor(
                            out=xT[:, oc, hsl(half)], in0=psG[a][:, :], scalar=modT[:, l, 16 + oc:17 + oc],
                            in1=xT[:, oc, hsl(half)], op0=ALU.mult, op1=ALU.add),
                            reads=[pk, "modT%d_%d" % (l, 4 + oc // 4), "xT%d_%d" % (oc, half)],
                            writes=["xT%d_%d" % (oc, half)])
            st["ring_order"] = [free_slot, sq_, s_kv]
            st["ring"] = 0

        xnorm_stats(0)
        for j in range(4):
            mod_part(0, j)
        for l in range(DEPTH):
            layer(l)

        final_stats_half(1)
        for c in range(8):
            final_apply_one(1, c)

        S.emit(nc, stack)
    return nc


POOL_WINDOWS = (2, 4, 8, 16)


def _pool_blocks(seq_len):
    L = seq_len
    blocks = np.zeros((32, 128, 128), np.float32)
    for g, w in enumerate(POOL_WINDOWS):
        A = np.zeros((1024, 1024), np.float32)
        for tt in range(1024):
            s0 = (tt // L) * L
            lo = max(tt - w // 2, s0)
            hi = min(tt + (w - w // 2), s0 + L)
            A[lo:hi, tt] = 1.0 / float(hi - lo)
            A[tt, tt] -= 1.0
        blocks[g] = A[0:128, 0:128]
        blocks[4 + g] = A[896:1024, 896:1024]
        blocks[8 + g] = A[256:384, 256:384]
        blocks[12 + g] = A[128:256, 128:256]
        blocks[16 + g] = A[0:128, 128:256]
        blocks[20 + g] = A[128:256, 0:128]
        blocks[24 + g] = A[128:256, 256:384]
        blocks[28 + g] = A[256:384, 128:256]
    return np.ascontiguousarray(blocks.transpose(1, 0, 2))


def _rope_tables(L, identity):
    if identity:
        cos = np.ones((L, 32), np.float32)
        sin = np.zeros((L, 32), np.float32)
    else:
        rows = L // 64
        row = np.repeat(np.arange(rows), 64).astype(np.float64)
        col = np.tile(np.arange(64), rows).astype(np.float64)
        inv = 1.0 / (10000.0 ** (np.arange(0, 32, 2, dtype=np.float64) / 32.0))
        ang = np.concatenate([row[:, None] * inv, col[:, None] * inv], axis=-1)
        cos, sin = np.cos(ang).astype(np.float32), np.sin(ang).astype(np.float32)
    cs = np.zeros((64, 2, L), np.float32)
    cs[0:32, 0] = cos.T
    cs[32:64, 0] = cos.T
    cs[0:32, 1] = -sin.T
    cs[32:64, 1] = sin.T
    return cs


_NC_CACHE = {}


def make_in_maps(x_prompt, x_sample, cache_ckv, cache_krope, c, c_ctx, w_mod, b_mod, g_norm, w_in,
                 g_q, w_uq, g_kv, w_ukv, pool_w, pool_s, conv_w, w_out, g_final):
    f = lambda a: np.ascontiguousarray(np.asarray(a, dtype=np.float32))
    x_prompt, x_sample, cache_ckv, cache_krope, c, c_ctx = map(f, (x_prompt, x_sample, cache_ckv, cache_krope, c, c_ctx))
    w_mod, b_mod, g_norm, w_in, g_q, w_uq, g_kv, w_ukv = map(f, (w_mod, b_mod, g_norm, w_in, g_q, w_uq, g_kv, w_ukv))
    pool_w, pool_s, conv_w, w_out, g_final = map(f, (pool_w, pool_s, conv_w, w_out, g_final))

    def pvec(v):
        v = np.asarray(v)
        lead = v.shape[:-1]
        C = v.shape[-1] // 128
        return np.ascontiguousarray(np.moveaxis(v.reshape(lead + (C, 128)), -1, 0))

    shared = {
        "w_mod": w_mod, "b_modT": pvec(b_mod), "g_normT": pvec(g_norm), "w_in": w_in, "g_qT": pvec(g_q),
        "w_uq": w_uq, "g_kvT": pvec(g_kv), "w_ukv": w_ukv, "pool_w": pool_w, "pool_sT": pvec(pool_s),
        "conv_wT": np.ascontiguousarray(np.transpose(conv_w.reshape(DEPTH, 3, 2, 128), (3, 0, 2, 1))),
        "w_out": w_out, "g_finalT": pvec(g_final),
    }
    blocks_p = _pool_blocks(256)
    blocks_s = _pool_blocks(1024)
    cs_p = _rope_tables(1024, True)
    cs_s = _rope_tables(1024, False)
    in_maps = []
    for core in range(8):
        m = dict(shared)
        if core < 4:
            xs = x_prompt[core * 4:(core + 1) * 4].reshape(T, D)
            m["xT"] = np.ascontiguousarray(xs.T)
            m["cvec"] = pvec(c_ctx)
            m["ctx_ckvT"] = np.zeros((DEPTH, 256, 512), np.float32)
            m["ctx_krT"] = np.zeros((DEPTH, 64, 512), np.float32)
            seq = np.arange(T) // 256
            mq = np.zeros((4, T), np.float32)
            mq[seq, np.arange(T)] = 1.0
            mk = np.full((4, NKEY), -BIG, np.float32)
            for j in range(4):
                mk[j, 512 + np.where(seq == j)[0]] = 0.0
            m["maskq"], m["maskk"] = mq, mk
            m["cs"] = cs_p
            m["poolA"] = blocks_p
            m["convfix"] = np.ones((128, 1), np.float32)
        else:
            b = core - 4
            m["xT"] = np.ascontiguousarray(x_sample[b].T)
            m["cvec"] = pvec(c[b])
            m["ctx_ckvT"] = np.ascontiguousarray(np.transpose(cache_ckv[b], (0, 2, 1)))
            m["ctx_krT"] = np.ascontiguousarray(np.transpose(cache_krope[b], (0, 2, 1)))
            mq = np.zeros((4, T), np.float32)
            mq[0] = 1.0
            m["maskq"], m["maskk"] = mq, np.zeros((4, NKEY), np.float32)
            m["cs"] = cs_s
            m["poolA"] = blocks_s
            m["convfix"] = np.zeros((128, 1), np.float32)
        in_maps.append(m)

    return in_maps


def kernel(**inputs):
    in_maps = make_in_maps(**inputs)
    if "nc" not in _NC_CACHE:
        _NC_CACHE["nc"] = build_program()
    nc = _NC_CACHE["nc"]
    res = run_bass_kernel_spmd(nc, in_maps, core_ids=list(range(8)))
    return assemble(res.results)


def assemble(R):
    y_prompt = np.zeros((16, 256, D), np.float32)
    y_sample = np.zeros((4, 1024, D), np.float32)
    state_ckv = np.zeros((16, DEPTH, 256, 256), np.float32)
    state_krope = np.zeros((16, DEPTH, 256, 64), np.float32)
    for core in range(8):
        yT = np.asarray(R[core]["yT"])
        if core < 4:
            y_prompt[core * 4:(core + 1) * 4] = yT.T.reshape(4, 256, D)
            ck = np.asarray(R[core]["ckvT_out"])
            kr = np.asarray(R[core]["krT_out"])
            state_ckv[core * 4:(core + 1) * 4] = np.transpose(ck.reshape(DEPTH, 256, 4, 256), (2, 0, 3, 1))
            state_krope[core * 4:(core + 1) * 4] = np.transpose(kr.reshape(DEPTH, 64, 4, 256), (2, 0, 3, 1))
        else:
            y_sample[core - 4] = yT.T
    return (y_prompt, y_sample, state_ckv, state_krope)
```
